# Optimizing a Trainium2 kernel written in Bass

```python
import math
import jax, jax.numpy as jnp
from jax import lax
import numpy as np

D_MODEL = 2048
BATCH = 8
SEQ = 2048
DEPTH = 1
DEC_BATCH = 2
DEC_SEQ = 16384
PAST_LEN = 128

SSM_EXPAND = 2
D_INNER = SSM_EXPAND * D_MODEL
SSM_HEAD_DIM = 64
SSM_HEADS = D_INNER // SSM_HEAD_DIM
SSM_GROUPS = 8
D_STATE = 128
GN = SSM_GROUPS * D_STATE
CONV_DIM = D_INNER + 2 * GN
CONV_WIDTH = 5
SSD_CHUNK = 128
ATTN_HEAD_DIM = 128
ATTN_HEADS = D_MODEL // ATTN_HEAD_DIM
KV_HEADS = 4
Q_PER_KV = ATTN_HEADS // KV_HEADS
WINDOW = 128
ATTN_BLOCK = 128
ATTN_SCALE = ATTN_HEAD_DIM ** -0.5
REL_BUCKETS = 32
REL_MAX_DIST = 128
D_FF = -(-8 * D_MODEL // (3 * 256)) * 256
EPS = 1e-6
Q_DIM = ATTN_HEADS * ATTN_HEAD_DIM
KV_DIM = KV_HEADS * ATTN_HEAD_DIM
IN_SPLITS = (D_INNER, CONV_DIM, 2 * SSM_HEADS, Q_DIM, KV_DIM, KV_DIM, 2 * D_MODEL)
IN_PROJ_DIM = D_INNER + CONV_DIM + 2 * SSM_HEADS + Q_DIM + 2 * KV_DIM + 2 * D_MODEL

kernel_name = 'hybrid_ssd_swa_encoder'


def rmsnorm(x, w):
    xf = x.astype(jnp.float32)
    y = xf * lax.rsqrt(jnp.mean(xf * xf, axis=-1, keepdims=True) + EPS) * w.astype(jnp.float32)
    return y.astype(x.dtype)


def centred_depthwise_conv(x, w, b):
    S = x.shape[1]
    half = CONV_WIDTH // 2
    xp = jnp.pad(x, ((0, 0), (half, half), (0, 0)))
    out = b
    for j in range(CONV_WIDTH):
        out = out + xp[:, j:j + S] * w[j]
    return out


def ssd_chunked_scan(x, dt, a, bm, cm):
    bsz, S, H, P = x.shape
    nc = S // SSD_CHUNK
    K = H // SSM_GROUPS

    def chunks(t):
        t = t.reshape((bsz, nc, SSD_CHUNK) + t.shape[2:])
        return jnp.moveaxis(t, 1, 0)

    xc = chunks(x.reshape(bsz, S, SSM_GROUPS, K, P))
    dtc = chunks(dt.reshape(bsz, S, SSM_GROUPS, K))
    ac = chunks((dt * a).reshape(bsz, S, SSM_GROUPS, K))
    bc, cc = chunks(bm), chunks(cm)
    lower = jnp.tril(jnp.ones((SSD_CHUNK, SSD_CHUNK), dtype=bool))

    def step(state, inp):
        xk, dtk, ak, bk, ck = inp
        acum = jnp.cumsum(ak, axis=1)
        xdt = xk * dtk[..., None]
        at = jnp.moveaxis(acum, 1, -1)
        seg = at[..., :, None] - at[..., None, :]
        decay = jnp.exp(jnp.where(lower, seg, -jnp.inf))
        cb = jnp.einsum('blgn,bsgn->bgls', ck, bk)
        y = jnp.einsum('bgls,bgkls,bsgkp->blgkp', cb, decay, xdt)
        y = y + jnp.einsum('blgn,bgkpn,blgk->blgkp', ck, state, jnp.exp(acum))
        last = acum[:, -1]
        w_in = jnp.exp(last[:, None] - acum)
        state = state * jnp.exp(last)[..., None, None] + jnp.einsum('bsgn,bsgk,bsgkp->bgkpn', bk, w_in, xdt)
        return state, y

    state0 = jnp.zeros((bsz, SSM_GROUPS, K, P, D_STATE), jnp.float32)
    _, y = lax.scan(step, state0, (xc, dtc, ac, bc, cc))
    return jnp.moveaxis(y, 0, 1).reshape(bsz, S, H, P)


def ssd_mixer(z, xbc, dt_raw, conv_w, conv_b, dt_bias, a_log, d_skip, norm_w, w_branch):
    bsz, S, _ = z.shape
    f32 = jnp.float32
    xbc = jax.nn.silu(centred_depthwise_conv(xbc, conv_w, conv_b)).astype(f32)
    xs = xbc[..., :D_INNER].reshape(bsz, S, SSM_HEADS, SSM_HEAD_DIM)
    bm = xbc[..., D_INNER:D_INNER + GN].reshape(bsz, S, SSM_GROUPS, D_STATE)
    cm = xbc[..., D_INNER + GN:].reshape(bsz, S, SSM_GROUPS, D_STATE)
    dt = jax.nn.softplus(dt_raw.astype(f32).reshape(bsz, S, 2, SSM_HEADS) + dt_bias.astype(f32))
    a = -jnp.exp(a_log.astype(f32))
    flip = lambda t: jnp.flip(t, axis=1)
    y_fwd = ssd_chunked_scan(xs, dt[:, :, 0], a[0], bm, cm)
    y_bwd = flip(ssd_chunked_scan(flip(xs), flip(dt[:, :, 1]), a[1], flip(bm), flip(cm)))
    y = y_fwd + y_bwd + xs * d_skip.astype(f32)[:, None]
    y = y.reshape(bsz, S, D_INNER) * jax.nn.silu(z.astype(f32))
    yg = y.reshape(bsz, S, SSM_GROUPS, D_INNER // SSM_GROUPS)
    yg = yg * lax.rsqrt(jnp.mean(yg * yg, axis=-1, keepdims=True) + EPS)
    y = yg.reshape(bsz, S, D_INNER) * norm_w.astype(f32)
    return y.astype(z.dtype) @ w_branch


def t5_buckets(rel):
    half = REL_BUCKETS // 2
    ret = (rel > 0).astype(np.int32) * half
    n = np.abs(rel)
    max_exact = half // 2
    large = max_exact + (np.log(np.maximum(n, 1) / max_exact) / np.log(REL_MAX_DIST / max_exact)
                         * (half - max_exact)).astype(np.int32)
    large = np.minimum(large, half - 1)
    return ret + np.where(n < max_exact, n, large).astype(np.int32)


def windowed_gqa(q, k, v, rel_bias, sink, w_branch):
    bsz, S, _ = q.shape
    nb = S // ATTN_BLOCK
    blk = ATTN_BLOCK
    q = q.reshape(bsz, nb, blk, KV_HEADS, Q_PER_KV, ATTN_HEAD_DIM)

    def band(t):
        t = t.reshape(bsz, S, KV_HEADS, ATTN_HEAD_DIM)
        t = jnp.pad(t, ((0, 0), (blk, blk), (0, 0), (0, 0))).reshape(bsz, nb + 2, blk, KV_HEADS, ATTN_HEAD_DIM)
        return jnp.concatenate([t[:, :-2], t[:, 1:-1], t[:, 2:]], axis=2)

    kw, vw = band(k), band(v)
    logits = jnp.einsum('bnqhrd,bnkhd->bnhrqk', q, kw).astype(jnp.float32) * ATTN_SCALE
    rel = np.arange(3 * blk)[None, :] - blk - np.arange(blk)[:, None]
    bias = rel_bias.astype(jnp.float32)[t5_buckets(rel)]
    bias = jnp.transpose(bias, (2, 0, 1)).reshape(KV_HEADS, Q_PER_KV, blk, 3 * blk)
    key_pos = (np.arange(nb)[:, None] - 1) * blk + np.arange(3 * blk)[None, :]
    valid = (np.abs(rel) <= WINDOW)[None] & ((key_pos >= 0) & (key_pos < S))[:, None, :]
    logits = jnp.where(valid[None, :, None, None], logits + bias[None, None], -jnp.inf)
    sink_l = sink.astype(jnp.float32).reshape(KV_HEADS, Q_PER_KV)[None, None, :, :, None, None]
    m = jnp.maximum(jnp.max(logits, axis=-1, keepdims=True), sink_l)
    p = jnp.exp(logits - m)
    probs = p / (jnp.sum(p, axis=-1, keepdims=True) + jnp.exp(sink_l - m))
    out = jnp.einsum('bnhrqk,bnkhd->bnqhrd', probs.astype(v.dtype), vw)
    return out.reshape(bsz, S, Q_DIM) @ w_branch


def trunk(x, mix_norm_w, w_in, conv_w, conv_b, dt_bias, a_log, d_skip, ssm_norm_w, w_ssm_branch,
          rel_bias, attn_sink, w_attn_branch, w_out, ffn_norm_w, w_ffn_in, w_ffn_out, final_norm_w):
    bsz, S, _ = x.shape
    h = x
    cuts = list(np.cumsum(IN_SPLITS)[:-1])
    for l in range(DEPTH):
        xn = rmsnorm(h, mix_norm_w[l])
        proj = xn @ w_in[l]
        z, xbc, dt_raw, q, k, v, gate_pre = jnp.split(proj, cuts, axis=-1)
        y_a = ssd_mixer(z, xbc, dt_raw, conv_w[l], conv_b[l], dt_bias[l], a_log[l], d_skip[l],
                        ssm_norm_w[l], w_ssm_branch[l])
        y_b = windowed_gqa(q, k, v, rel_bias, attn_sink[l], w_attn_branch[l])
        g = jax.nn.sigmoid(gate_pre.astype(jnp.float32)).reshape(bsz, S, 2, D_MODEL)
        merged = (g[:, :, 0] * y_a.astype(jnp.float32) + g[:, :, 1] * y_b.astype(jnp.float32)).astype(x.dtype)
        h = h + merged @ w_out[l]
        hn = rmsnorm(h, ffn_norm_w[l])
        gt, up = jnp.split(hn @ w_ffn_in[l], 2, axis=-1)
        h = h + (jax.nn.silu(gt) * up) @ w_ffn_out[l]
    return rmsnorm(h, final_norm_w)


def setup_inputs(seed: int = 0) -> dict:
    key = jax.random.key(seed)
    ks = jax.random.split(key, 20)
    f32 = jnp.float32
    nrm = lambda k, shape, s: jax.random.normal(k, shape, f32) * s
    u = jax.random.uniform(ks[6], (DEPTH, 2, SSM_HEADS), f32)
    dt0 = jnp.exp(u * (math.log(0.1) - math.log(0.001)) + math.log(0.001))
    dt_bias = dt0 + jnp.log(-jnp.expm1(-dt0))
    return {
        'x_prompt': nrm(ks[0], (BATCH, SEQ, D_MODEL), 1.0),
        'x_sample': nrm(ks[1], (DEC_BATCH, DEC_SEQ, D_MODEL), 1.0),
        'mix_norm_w': 1.0 + nrm(ks[2], (DEPTH, D_MODEL), 0.02),
        'w_in': nrm(ks[3], (DEPTH, D_MODEL, IN_PROJ_DIM), D_MODEL ** -0.5),
        'conv_w': nrm(ks[4], (DEPTH, CONV_WIDTH, CONV_DIM), CONV_WIDTH ** -0.5),
        'conv_b': nrm(ks[5], (DEPTH, CONV_DIM), 0.01),
        'dt_bias': dt_bias,
        'a_log': jnp.log(jax.random.uniform(ks[7], (DEPTH, 2, SSM_HEADS), f32, 1.0, 16.0)),
        'd_skip': 1.0 + nrm(ks[8], (DEPTH, SSM_HEADS), 0.02),
        'ssm_norm_w': 1.0 + nrm(ks[9], (DEPTH, D_INNER), 0.02),
        'w_ssm_branch': nrm(ks[10], (DEPTH, D_INNER, D_MODEL), D_INNER ** -0.5),
        'rel_bias': nrm(ks[11], (REL_BUCKETS, ATTN_HEADS), 0.5),
        'attn_sink': nrm(ks[12], (DEPTH, ATTN_HEADS), 0.5),
        'w_attn_branch': nrm(ks[13], (DEPTH, Q_DIM, D_MODEL), Q_DIM ** -0.5),
        'w_out': nrm(ks[14], (DEPTH, D_MODEL, D_MODEL), D_MODEL ** -0.5),
        'ffn_norm_w': 1.0 + nrm(ks[15], (DEPTH, D_MODEL), 0.02),
        'w_ffn_in': nrm(ks[16], (DEPTH, D_MODEL, 2 * D_FF), D_MODEL ** -0.5),
        'w_ffn_out': nrm(ks[17], (DEPTH, D_FF, D_MODEL), D_FF ** -0.5),
        'final_norm_w': 1.0 + nrm(ks[18], (D_MODEL,), 0.02),
    }


def reference(x_prompt, x_sample, mix_norm_w, w_in, conv_w, conv_b, dt_bias, a_log, d_skip, ssm_norm_w,
              w_ssm_branch, rel_bias, attn_sink, w_attn_branch, w_out, ffn_norm_w, w_ffn_in, w_ffn_out,
              final_norm_w):
    y_prompt = trunk(x_prompt, mix_norm_w, w_in, conv_w, conv_b, dt_bias, a_log, d_skip, ssm_norm_w,
                     w_ssm_branch, rel_bias, attn_sink, w_attn_branch, w_out, ffn_norm_w, w_ffn_in,
                     w_ffn_out, final_norm_w)
    y_sample = trunk(x_sample, mix_norm_w, w_in, conv_w, conv_b, dt_bias, a_log, d_skip, ssm_norm_w,
                     w_ssm_branch, rel_bias, attn_sink, w_attn_branch, w_out, ffn_norm_w, w_ffn_in,
                     w_ffn_out, final_norm_w)
    return (y_prompt, y_sample)
```

```python
import math
from contextlib import ExitStack
from types import SimpleNamespace
import numpy as np
import concourse.bass as bass
import concourse.mybir as mybir
from concourse.bass_utils import run_bass_kernel_spmd

F32 = mybir.dt.float32
BF16 = mybir.dt.bfloat16
AF = mybir.ActivationFunctionType
ALU = mybir.AluOpType
AX = mybir.AxisListType
EPS = 1e-6
NEG = -30000.0
P = 128


def make_cfg(D, SEQ, DEC_SEQ, G):
    c = SimpleNamespace()
    c.D = D; c.DI = 2 * D; c.H = c.DI // 64; c.G = G; c.K = c.H // G
    assert c.K == 8
    c.GN = G * 128; c.CD = c.DI + 2 * c.GN; c.CT = c.CD // 128
    c.AH = D // 128; c.KVH = c.AH // 4; c.QD = D; c.KVD = c.KVH * 128
    c.DFF = -(-8 * D // (3 * 256)) * 256
    c.H2 = 2 * c.H
    c.IN = c.DI + c.CD + c.H2 + c.QD + 2 * c.KVD + 2 * D
    c.SEQ = SEQ; c.DEC_SEQ = DEC_SEQ; c.QL = DEC_SEQ // 4
    c.TA = SEQ // 128; c.TB = c.QL // 128
    assert c.TA % 4 == 0 and c.TB % 4 == 0
    c.NT = c.TA + c.TB + 4; c.NTOK = c.NT * 128
    c.ownA = list(range(1, c.TA + 1)); c.ownB = list(range(c.TA + 3, c.TA + c.TB + 3))
    c.own = c.ownA + c.ownB
    c.oz = 0; c.oxbc = c.DI; c.odt = c.DI + c.CD; c.oq = c.odt + c.H2
    c.ok = c.oq + c.QD; c.ov = c.ok + c.KVD; c.og = c.ov + c.KVD
    c.debug = False
    return c


CFG_FULL = make_cfg(2048, 2048, 16384, 8)


class Buf:
    __slots__ = ("w", "r")

    def __init__(self):
        self.w = {}
        self.r = {}


class Sched:
    ENGS = ("pe", "act", "dve", "pool", "sp")

    def __init__(self, nc, es):
        self.nc = nc
        self.es = es
        self.q = {e: [] for e in self.ENGS}
        self.sem = {}
        self.tot = {}
        self.seen = {e: {} for e in self.ENGS}
        for e in self.ENGS:
            self.sem[e] = es.enter_context(nc.semaphore("s_" + e))
            self.tot[e] = 0
        self.nd = 0

    def dsem(self):
        k = "d%d" % self.nd
        self.nd += 1
        self.sem[k] = self.es.enter_context(self.nc.semaphore("s_" + k))
        self.tot[k] = 0
        return k

    def _waits(self, eng, reads, writes):
        need = {}
        for b in reads:
            for k, v in b.w.items():
                if need.get(k, 0) < v:
                    need[k] = v
        for b in writes:
            for dct in (b.w, b.r):
                for k, v in dct.items():
                    if need.get(k, 0) < v:
                        need[k] = v
        out = []
        seen = self.seen[eng]
        for k, v in need.items():
            if k == "pe" and eng == "pe":
                continue
            if k[0] == "d":
                v = self.tot[k]
            if seen.get(k, 0) < v:
                seen[k] = v
                out.append((k, v))
        return out

    def _mark(self, key, val, reads, writes):
        for b in writes:
            b.w = {key: val}
            b.r = {}
        for b in reads:
            if b.r.get(key, 0) < val:
                b.r[key] = val

    def op(self, eng, fn, reads=(), writes=()):
        waits = self._waits(eng, reads, writes)
        self.tot[eng] += 1
        self.q[eng].append((waits, fn, eng, 1))
        self._mark(eng, self.tot[eng], reads, writes)

    def dma(self, eng, out, in_, reads, writes, ds, slow=False):
        waits = self._waits(eng, reads, writes)
        self.tot[ds] += 16
        if slow:
            fn = lambda e: e.dma_start(out=out, in_=in_, allow_slow_non_contiguous=True)
        else:
            fn = lambda e: e.dma_start(out=out, in_=in_)
        self.q[eng].append((waits, fn, ds, 16))
        self._mark(ds, self.tot[ds], reads, writes)

    def mm(self, out, pairs, first, last, reads, writes):
        pairs = list(pairs)
        n = len(pairs)

        def fn(e):
            ins = None
            for i, (l, r) in enumerate(pairs):
                ins = e.matmul(out, l, r, start=(first and i == 0), stop=(last and i == n - 1))
            return ins
        self.op("pe", fn, reads, writes)

    def mmv(self, items, reads, writes):
        items = list(items)

        def fn(e):
            ins = None
            for (o, l, r, st, sp) in items:
                ins = e.matmul(o, l, r, start=st, stop=sp)
            return ins
        self.op("pe", fn, reads, writes)

    def tr(self, items, ident, reads, writes):
        items = list(items)

        def fn(e):
            ins = None
            for (o, i) in items:
                ins = e.transpose(out=o, in_=i, identity=ident)
            return ins
        self.op("pe", fn, reads, writes)

    def act(self, out, in_, func, reads, writes, **kw):
        self.op("act", lambda e: e.activation(out=out, in_=in_, func=func, **kw), reads, writes)

    def copy(self, eng, out, in_, reads, writes):
        if eng == "act":
            self.op("act", lambda e: e.copy(out=out, in_=in_), reads, writes)
        else:
            self.op(eng, lambda e: e.tensor_copy(out=out, in_=in_), reads, writes)

    def tt(self, eng, out, in0, in1, op, reads, writes):
        self.op(eng, lambda e: e.tensor_tensor(out=out, in0=in0, in1=in1, op=op), reads, writes)

    def ts(self, eng, out, in0, s1, s2, op0, op1, reads, writes):
        if s2 is None:
            self.op(eng, lambda e: e.tensor_scalar(out=out, in0=in0, scalar1=s1, scalar2=0.0, op0=op0, op1=ALU.add), reads, writes)
        else:
            self.op(eng, lambda e: e.tensor_scalar(out=out, in0=in0, scalar1=s1, scalar2=s2, op0=op0, op1=op1), reads, writes)

    def stt(self, eng, out, in0, scalar, in1, op0, op1, reads, writes):
        self.op(eng, lambda e: e.scalar_tensor_tensor(out=out, in0=in0, scalar=scalar, in1=in1, op0=op0, op1=op1),
                reads, writes)

    def memset(self, eng, ap, val, writes):
        self.op(eng, lambda e: e.memset(ap, val), [], writes)

    def coll(self, fn, reads, writes):
        k = "c%d" % self.nd
        self.nd += 1
        self.sem[k] = self.es.enter_context(self.nc.semaphore("s_" + k))
        self.tot[k] = 1
        waits = self._waits("pool", reads, writes)
        self.q["pool"].append((waits, fn, k, 1))
        self._mark(k, 1, reads, writes)

    def dma3(self, eng, out, in_, reads, writes, ds, step=8):
        n = out.shape[1]
        for j in range(0, n, step):
            k = min(step, n - j)
            self.dma(eng, out[:, j:j + k, :], in_[:, j:j + k, :], reads, writes, ds)

    def barrier(self):
        for e in self.ENGS:
            waits = []
            for k, v in self.tot.items():
                if k == e or v == 0:
                    continue
                if self.seen[e].get(k, 0) < v:
                    self.seen[e][k] = v
                    waits.append((k, v))
            if waits:
                self.q[e].append((waits, None, None, 0))

    def flush(self):
        nc = self.nc
        engobj = {"pe": "tensor", "act": "scalar", "dve": "vector", "pool": "gpsimd", "sp": "sync"}
        with nc.Block() as block:
            for e in self.ENGS:
                items = self.q[e]
                if not items:
                    continue

                def body(eng, items=items):
                    for waits, fn, key, inc in items:
                        for k, v in waits:
                            eng.wait_ge(self.sem[k], v)
                        if fn is not None:
                            ins = fn(eng)
                            ins.then_inc(self.sem[key], inc)

                getattr(block, engobj[e])(body)
        self.q = {e: [] for e in self.ENGS}


def bc(ap, shape):
    return ap.to_broadcast(list(shape))


def build(cfg):
    c = cfg
    nc = bass.Bass("TRN2", target_bir_lowering=False)
    D, DI, H, H2, G, CT, CD = c.D, c.DI, c.H, c.H2, c.G, c.CT, c.CD
    NT, NTOK = c.NT, c.NTOK
    KT = D // 128

    def din(name, shape, dt=F32):
        return nc.dram_tensor(name, list(shape), dt, kind="ExternalInput")

    skind = "ExternalOutput" if c.debug else "Internal"

    def dscr(name, shape, dt):
        if c.debug:
            return nc.dram_tensor(name, list(shape), dt, kind="ExternalOutput")
        return nc.dram_tensor(name, list(shape), dt)

    xin = din("xin", [NTOK, D])
    w_in = din("w_in", [D, c.IN]); w_ssm = din("w_ssm", [DI, D]); w_attn = din("w_attn", [D, D])
    w_out = din("w_out", [D, D]); w_fi = din("w_fi", [D, 2 * c.DFF]); w_fo = din("w_fo", [c.DFF, D])
    normw = din("normw", [3, D])
    conv_w = din("conv_w", [128, CT * 5]); conv_b = din("conv_b", [128, CT])
    dtb = din("dtb", [1, H2]); alog = din("alog", [1, H2]); dskip = din("dskip", [1, H])
    snw = din("snw", [1, DI]); relb = din("relb", [1, 32 * c.AH]); sink = din("sink", [1, c.AH])
    sel = din("sel", [1, 8]); hval = din("hval", [1, 4])
    cst = din("cst", [128, 6 * 128])
    ohm = din("ohm", [128, 33 * 384])
    y_out = nc.dram_tensor("y_out", [len(c.own) * 128, D], F32, kind="ExternalOutput")

    wb_in = nc.dram_tensor("wb_in", [D, c.IN], BF16); wb_ssm = nc.dram_tensor("wb_ssm", [DI, D], BF16)
    wb_attn = nc.dram_tensor("wb_attn", [D, D], BF16); wb_out = nc.dram_tensor("wb_out", [D, D], BF16)
    wb_fi = nc.dram_tensor("wb_fi", [D, 2 * c.DFF], BF16); wb_fo = nc.dram_tensor("wb_fo", [c.DFF, D], BF16)
    zs = dscr("zs", [NTOK, DI], BF16); vs = dscr("vs", [NTOK, c.KVD], BF16); dts = dscr("dts", [NTOK, H2], F32)
    xbcT = dscr("xbcT", [CD, NTOK], F32); qT = dscr("qT", [c.QD, NTOK], BF16); kT = dscr("kT", [c.KVD, NTOK], BF16)
    gT = dscr("gT", [2 * D, NTOK], BF16)
    xcsT = dscr("xcsT", [CD, NTOK], BF16)
    Ls = dscr("Ls", [2 * NT * 128, DI], F32); etot = dscr("etot", [NT, H2], F32)
    Sin = dscr("Sin", [2 * NT * 128, DI], BF16)
    agin = nc.dram_tensor("agin", [257, DI], F32); agout = nc.dram_tensor("agout", [8 * 257, DI], F32)
    ynT = dscr("ynT", [DI, NTOK], BF16); atT = dscr("atT", [c.QD, NTOK], BF16)

    es = ExitStack()
    S = Sched(nc, es)

    uid = [0]

    def sb(ps, name, shape, dt):
        uid[0] += 1
        return ps.enter_context(nc.sbuf_tensor("%s_%d" % (name, uid[0]), list(shape), dt))

    with ExitStack() as ps:
        NBUF = 3
        CW = 4096
        ld = [sb(ps, "ld%d" % i, [128, CW], F32) for i in range(NBUF)]; ldB = [Buf() for _ in range(NBUF)]
        cv = [sb(ps, "cv%d" % i, [128, CW], BF16) for i in range(NBUF)]; cvB = [Buf() for _ in range(NBUF)]
        lds = [S.dsem() for _ in range(NBUF)]; cvs = [S.dsem() for _ in range(NBUF)]
        n = 0
        for src, dst, rows, cols in ((w_in, wb_in, D, c.IN), (w_ssm, wb_ssm, DI, D), (w_attn, wb_attn, D, D),
                                     (w_out, wb_out, D, D), (w_fi, wb_fi, D, 2 * c.DFF), (w_fo, wb_fo, c.DFF, D)):
            for r0 in range(0, rows, 128):
                for c0 in range(0, cols, CW):
                    cw_ = min(CW, cols - c0)
                    i = n % NBUF
                    S.dma("sp", ld[i][:, 0:cw_], src[r0:r0 + 128, c0:c0 + cw_], [], [ldB[i]], lds[i])
                    S.copy(("act", "dve", "pool")[n % 3], cv[i][:, 0:cw_], ld[i][:, 0:cw_], [ldB[i]], [cvB[i]])
                    S.dma("act", dst[r0:r0 + 128, c0:c0 + cw_], cv[i][:, 0:cw_], [cvB[i]], [], cvs[i])
                    n += 1
        S.barrier()
        S.flush()

    gs = ExitStack()
    csb = sb(gs, "csb", [128, 768], F32)
    identb = sb(gs, "identb", [128, 128], BF16)
    psf = [gs.enter_context(nc.psum_tensor("psf%d" % i, [128, 512], F32)) for i in range(8)]
    psfB = [Buf() for _ in psf]
    psbv = [p[:, :].bitcast(BF16) for p in psf]
    rr = {"f": 0}

    def bankf():
        i = rr["f"] % len(psf); rr["f"] += 1
        return psf[i], psfB[i]

    def bankb():
        i = rr["f"] % len(psf); rr["f"] += 1
        return psbv[i], psfB[i]

    ident = csb[:, 0:128]; U = csb[:, 128:256]; UT = csb[:, 256:384]
    SU = csb[:, 384:512]; SUT = csb[:, 512:640]; ones = csb[:, 640:768]
    Bc = Buf()
    dc = S.dsem()
    S.dma("sp", csb[:, :], cst[:, :], [], [Bc], dc)
    S.op("dve", lambda e: e.tensor_copy(out=identb[:, :], in_=ident), [Bc], [Bc])
    epsb = sb(gs, "epsb", [128, 1], F32)
    S.op("pool", lambda e: e.memset(epsb[:, :], EPS), [], [Bc])

    WKT = 8

    class WStream:
        def __init__(self, ps, nslots=6):
            self.t = [sb(ps, "wp%d" % i, [128, WKT, 512], BF16) for i in range(nslots)]
            self.b = [Buf() for _ in range(nslots)]
            self.ds = [S.dsem() for _ in range(nslots)]
            self.i = 0

        def load(self, wsrc, k0, nk, c0, wd):
            i = self.i % len(self.t); self.i += 1
            src = wsrc[k0 * 128:(k0 + nk) * 128, c0:c0 + wd].rearrange("(kt p) n -> p kt n", p=128)
            S.dma("sp", self.t[i][:, 0:nk, 0:wd], src, [], [self.b[i]], self.ds[i])
            return self.t[i], self.b[i]

    def rstd_ops(ap, B, eps=EPS):
        S.act(ap, ap, AF.Ln, [Bc], [B], bias=epsb[:, 0:1], scale=1.0)
        S.act(ap, ap, AF.Exp, [], [B], scale=-0.5)

    def transpose_to(src2d_fn, nblk, dst_fn, srcB, dstB):
        for j0 in range(0, nblk, 8):
            n = min(8, nblk - j0)
            pb, pbB = bankb()
            S.tr([(pb[:, i * 128:(i + 1) * 128], src2d_fn(j0 + i)) for i in range(n)], identb[:, :], [srcB, Bc], [pbB])
            eng = "act" if (j0 // 8) % 2 == 0 else "dve"
            S.copy(eng, dst_fn(j0, n), pb[:, 0:n * 128].rearrange("p (k t) -> p k t", t=128), [pbB], [dstB])

    def rmsnorm_T(ps_tiles, src, srcB, nw, dstT, dstB, tmpb, tmpB, ss, ssB, ntile):
        sq, sqB = ps_tiles
        for tt in range(ntile):
            S.act(sq[:, :], src[:, tt, :], AF.Square, [srcB], [sqB, ssB], scale=float(D) ** -0.5,
                  accum_out=ss[:, tt:tt + 1])
        rstd_ops(ss[:, 0:ntile], ssB)
        for tt in range(ntile):
            S.stt("dve", tmpb[:, tt, :], src[:, tt, :], ss[:, tt:tt + 1], nw[0], ALU.mult, ALU.mult,
                  [srcB, ssB, nw[1]], [tmpB])
        for tt in range(ntile):
            transpose_to(lambda j, tt=tt: tmpb[:, tt, j * 128:(j + 1) * 128], KT,
                         lambda j0, n, tt=tt: dstT[:, j0:j0 + n, tt * 128:(tt + 1) * 128], tmpB, dstB)

    NB = 4
    with ExitStack() as ps:
        xt = [sb(ps, "xt%d" % i, [128, NB, D], F32) for i in range(2)]
        xtB = [Buf(), Buf()]; xds = [S.dsem(), S.dsem()]
        sq = sb(ps, "sq", [128, D], F32); sqB = Buf()
        ss = sb(ps, "ss", [128, 8], F32); ssB = Buf()
        tmpb = sb(ps, "tmpb", [128, NB, D], BF16); tmpB = Buf()
        xnT = sb(ps, "xnT", [128, KT, NB * 128], BF16); xnB = Buf()
        stf = [sb(ps, "stf%d" % i, [128, 4, 512], F32) for i in range(2)]
        stb = [sb(ps, "stb%d" % i, [128, 4, 512], BF16) for i in range(2)]
        stfB = [Buf(), Buf()]; stbB = [Buf(), Buf()]
        sds = [S.dsem() for _ in range(4)]
        W = WStream(ps, 8)
        nwa = sb(ps, "nwa", [128, D], F32); nwaB = Buf(); nwd = S.dsem()
        S.dma("sp", nwa[:, :], normw[0:1, :].partition_broadcast(128), [], [nwaB], nwd)
        chunks = []
        for j in range(0, DI, 512):
            chunks.append((c.oz + j, 512, "tok", zs, j, "b"))
        for j in range(0, CD, 512):
            chunks.append((c.oxbc + j, 512, "feat", xbcT, j, "f"))
        chunks.append((c.odt, H2, "tok", dts, 0, "f"))
        for j in range(0, c.QD, 512):
            chunks.append((c.oq + j, 512, "feat", qT, j, "b"))
        for j in range(0, c.KVD, 512):
            wd = min(512, c.KVD - j)
            chunks.append((c.ok + j, wd, "feat", kT, j, "b"))
        for j in range(0, c.KVD, 512):
            wd = min(512, c.KVD - j)
            chunks.append((c.ov + j, wd, "tok", vs, j, "b"))
        for j in range(0, 2 * D, 512):
            chunks.append((c.og + j, 512, "feat", gT, j, "g"))
        nst = [0, 0]
        nblk = NT // NB
        for b in range(nblk):
            xi = b % 2
            S.dma("sp", xt[xi][:, :, :], xin[b * NB * 128:(b + 1) * NB * 128, :].rearrange("(t p) d -> p t d", p=128),
                  [], [xtB[xi]], xds[xi])
            rmsnorm_T((sq, sqB), xt[xi], xtB[xi], (nwa[:, :], nwaB), xnT, xnB, tmpb, tmpB, ss, ssB, NB)
            for (c0, wd, mode, dst, doff, kind) in chunks:
                panels = []
                for k0 in range(0, KT, WKT):
                    nk = min(WKT, KT - k0)
                    panels.append((k0, nk) + W.load(wb_in, k0, nk, c0, wd))
                if kind == "f":
                    si = nst[0] % 2; nst[0] += 1
                    st, stB, sd = stf[si], stfB[si], sds[si]
                else:
                    si = nst[1] % 2; nst[1] += 1
                    st, stB, sd = stb[si], stbB[si], sds[2 + si]
                nsub = 4 if mode == "tok" else wd // 128
                ncol = wd if mode == "tok" else NB * 128
                for u in range(nsub):
                    pf, pfB = bankf()
                    for pi, (k0, nk, wt, wB) in enumerate(panels):
                        if mode == "tok":
                            pairs = [(xnT[:, k0 + j, u * 128:(u + 1) * 128], wt[:, j, 0:wd]) for j in range(nk)]
                        else:
                            pairs = [(wt[:, j, u * 128:(u + 1) * 128], xnT[:, k0 + j, :]) for j in range(nk)]
                        S.mm(pf[:, 0:ncol], pairs, pi == 0, pi == len(panels) - 1, [xnB, wB], [pfB])
                    if kind == "g":
                        S.act(st[:, u, 0:ncol], pf[:, 0:ncol], AF.Sigmoid, [pfB], [stB])
                    else:
                        S.copy("act" if u % 2 == 0 else "dve", st[:, u, 0:ncol], pf[:, 0:ncol], [pfB], [stB])
                t0 = b * NB * 128
                if mode == "tok":
                    S.dma("act", dst[t0:t0 + NB * 128, doff:doff + wd].rearrange("(t p) n -> p t n", p=128),
                          st[:, 0:4, 0:wd], [stB], [], sd)
                else:
                    S.dma("act", dst[doff:doff + wd, t0:t0 + NB * 128].rearrange("(u p) t -> p u t", p=128),
                          st[:, 0:nsub, 0:NB * 128], [stB], [], sd)
        S.barrier()
        S.flush()

    ctx = SimpleNamespace(**locals())
    build_ssd_attn(ctx)
    build_phase_c(ctx)
    S.barrier()
    S.flush()
    gs.close()
    es.close()
    return nc


def build_ssd_attn(x):
    S, nc, c = x.S, x.nc, x.c
    sb, bankf, bankb, Bc, transpose_to = x.sb, x.bankf, x.bankb, x.Bc, x.transpose_to
    D, DI, H, H2, G, CT, CD, NT = c.D, c.DI, c.H, c.H2, c.G, c.CT, c.CD, c.NT
    DIt = DI // 128
    U, UT, SU, SUT, ones = x.U, x.UT, x.SU, x.SUT, x.ones
    U3 = x.csb[:, 128:256].rearrange("p (o l) -> p o l", o=1)
    UT3 = x.csb[:, 256:384].rearrange("p (o l) -> p o l", o=1)

    def load_small(ps):
        t = SimpleNamespace()
        t.B = Buf()
        ds = S.dsem()
        t.dtb = sb(ps, "dtbb", [128, H2], F32); t.A = sb(ps, "Abc", [128, H2], F32)
        t.oneb = sb(ps, "oneb", [128, 1], F32)
        S.dma("sp", t.dtb[:, :], x.dtb[0:1, :].partition_broadcast(128), [], [t.B], ds)
        S.dma("sp", t.A[:, :], x.alog[0:1, :].partition_broadcast(128), [], [t.B], ds)
        S.act(t.A[:, :], t.A[:, :], AF.Exp, [], [t.B])
        S.ts("dve", t.A[:, :], t.A[:, :], -1.0, None, ALU.mult, None, [], [t.B])
        S.memset("pool", t.oneb[:, :], 1.0, [t.B])
        t.dtr = sb(ps, "dtr", [128, H2], F32); t.vv = sb(ps, "vv", [128, H2], F32); t.ab = sb(ps, "ab", [128, H2], F32)
        t.dtv = sb(ps, "dtv", [128, H2, 1], F32); t.a3 = sb(ps, "a3", [128, H2, 1], F32)
        t.cs4 = sb(ps, "cs4", [128, 2 * H2], F32)
        t.dB = Buf(); t.dds = S.dsem()
        return t

    def dt_chunk(t, tile):
        S.dma("sp", t.dtr[:, :], x.dts[tile * 128:(tile + 1) * 128, :], [], [t.dB], t.dds)
        S.tt("dve", t.vv[:, :], t.dtr[:, :], t.dtb[:, :], ALU.add, [t.B], [t.dB])
        S.ts("dve", t.ab[:, :], t.vv[:, :], -1.0, None, ALU.mult, None, [], [t.dB])
        S.tt("dve", t.ab[:, :], t.ab[:, :], t.vv[:, :], ALU.max, [], [t.dB])
        S.act(t.ab[:, :], t.ab[:, :], AF.Exp, [], [t.dB], scale=-1.0)
        S.act(t.ab[:, :], t.ab[:, :], AF.Ln, [t.B], [t.dB], bias=t.oneb[:, 0:1], scale=1.0)
        S.ts("dve", t.vv[:, :], t.vv[:, :], 0.0, None, ALU.max, None, [], [t.dB])
        S.tt("dve", t.dtv[:, :, 0], t.vv[:, :], t.ab[:, :], ALU.add, [], [t.dB])
        S.tt("dve", t.a3[:, :, 0], t.dtv[:, :, 0], t.A[:, :], ALU.mult, [t.B], [t.dB])
        pf, pfB = bankf()
        S.mmv([(pf[:, 0:H], U, t.a3[:, 0:H, 0], True, True), (pf[:, H:H2], UT, t.a3[:, H:H2, 0], True, True),
               (pf[:, H2:2 * H2], ones, t.a3[:, :, 0], True, True)], [t.dB, Bc], [pfB])
        S.copy("dve", t.cs4[:, :], pf[:, 0:2 * H2], [pfB], [t.dB])

    with ExitStack() as ps:
        t = load_small(ps)
        cw = sb(ps, "cw", [128, CT, 5], F32); cb = sb(ps, "cb", [128, CT, 1], F32)
        ds = S.dsem()
        S.dma("sp", cw[:, :, :], x.conv_w[:, :].rearrange("p (ct j) -> p ct j", j=5), [], [t.B], ds)
        S.dma("sp", cb[:, :, 0], x.conv_b[:, :], [], [t.B], ds)
        xcs_ = [sb(ps, "xc%d" % i, [128, CT, 132], F32) for i in range(2)]; xcBs = [Buf(), Buf()]; xcds = [S.dsem(), S.dsem()]
        acc = sb(ps, "acc", [128, CT, 128], F32); tmpc = sb(ps, "tmpc", [128, CT, 128], F32)
        CA = (CT * 2) // 3
        accBs = [Buf(), Buf()]; tmpcBs = [Buf(), Buf()]
        xss = [sb(ps, "xs%d" % i, [128, CT, 128], BF16) for i in range(2)]; xsBs = [Buf(), Buf()]; xsds = [S.dsem(), S.dsem()]
        xtok = sb(ps, "xtok", [128, DI], BF16); xtokB = Buf()
        Btok = sb(ps, "Btok", [128, G, 128], BF16); BtokB = Buf()
        xw = [sb(ps, "xw%d" % d, [128, DI], BF16) for d in range(2)]; xwB = [Buf(), Buf()]
        Lst = [sb(ps, "Lst%d" % d, [128, DI], F32) for d in range(2)]; LstB = [Buf(), Buf()]; Lds = [S.dsem(), S.dsem()]
        wt_ = sb(ps, "wtmp", [128, H2], F32); dw3 = sb(ps, "dw3", [128, H2, 1], F32); et = sb(ps, "et", [128, H2], F32)
        wB = Buf(); etB = Buf(); etd = S.dsem()
        for ti, tile in enumerate(c.own):
            xc = xcs_[ti % 2]; xcB = xcBs[ti % 2]; xs = xss[ti % 2]; xsB = xsBs[ti % 2]
            S.dma3("sp", xc[:, :, :], x.xbcT[:, tile * 128 - 2:tile * 128 + 130].rearrange("(ct p) t -> p ct t", p=128),
                   [], [xcB], xcds[ti % 2])
            for half, (eng, c0_, c1_) in enumerate((("dve", 0, CA), ("pool", CA, CT))):
                shp = [128, c1_ - c0_, 128]
                aB = accBs[half]; tB_ = tmpcBs[half]
                S.tt(eng, acc[:, c0_:c1_, :], xc[:, c0_:c1_, 0:128], bc(cw[:, c0_:c1_, 0:1], shp), ALU.mult, [xcB, t.B], [aB])
                for j in range(1, 5):
                    S.tt(eng, tmpc[:, c0_:c1_, :], xc[:, c0_:c1_, j:j + 128], bc(cw[:, c0_:c1_, j:j + 1], shp), ALU.mult,
                         [xcB, t.B], [tB_])
                    S.tt(eng, acc[:, c0_:c1_, :], acc[:, c0_:c1_, :], tmpc[:, c0_:c1_, :], ALU.add, [tB_], [aB])
                S.tt(eng, acc[:, c0_:c1_, :], acc[:, c0_:c1_, :], bc(cb[:, c0_:c1_, :], shp), ALU.add, [t.B], [aB])
            S.act(xs[:, :, :], acc[:, :, :], AF.Silu, accBs, [xsB])
            S.dma3("act", x.xcsT[:, tile * 128:(tile + 1) * 128].rearrange("(ct p) t -> p ct t", p=128), xs[:, :, :],
                   [xsB], [], xsds[ti % 2])
            transpose_to(lambda j: xs[:, j, :], DIt,
                         lambda j0, n: xtok[:, j0 * 128:(j0 + n) * 128].rearrange("p (k t) -> p k t", t=128), xsB, xtokB)
            transpose_to(lambda g: xs[:, DIt + g, :], G, lambda j0, n: Btok[:, j0:j0 + n, :], xsB, BtokB)
            dt_chunk(t, tile)
            S.tt("dve", wt_[:, :], t.cs4[:, H2:2 * H2], t.cs4[:, 0:H2], ALU.subtract, [t.dB], [wB])
            S.act(wt_[:, :], wt_[:, :], AF.Exp, [], [wB])
            S.tt("dve", dw3[:, :, 0], wt_[:, :], t.dtv[:, :, 0], ALU.mult, [t.dB], [wB])
            S.act(et[:, :], t.cs4[:, H2:2 * H2], AF.Exp, [t.dB], [etB])
            S.dma("act", x.etot[tile:tile + 1, :], et[0:1, :], [etB], [], etd)
            for d in range(2):
                S.tt("dve" if d == 0 else "pool", xw[d][:, :].rearrange("p (h q) -> p h q", q=64),
                     xtok[:, :].rearrange("p (h q) -> p h q", q=64), bc(dw3[:, d * H:(d + 1) * H, :], [128, H, 64]),
                     ALU.mult, [xtokB, wB], [xwB[d]])
                for g in range(G):
                    pf, pfB = bankf()
                    S.mm(pf[:, 0:512], [(Btok[:, g, :], xw[d][:, g * 512:(g + 1) * 512])], True, True, [BtokB, xwB[d]], [pfB])
                    S.copy("act" if g % 2 == 0 else "dve", Lst[d][:, g * 512:(g + 1) * 512], pf[:, 0:512], [pfB], [LstB[d]])
                r0 = (d * NT + tile) * 128
                S.dma("act", x.Ls[r0:r0 + 128, :], Lst[d][:, :], [LstB[d]], [], Lds[d])
        S.barrier()
        S.flush()

    with ExitStack() as ps:
        St = sb(ps, "St", [128, DI], F32); StB = Buf()
        accs = sb(ps, "accs", [128, DI], F32); accsB = Buf()
        Lt = [sb(ps, "Lt%d" % i, [128, DI], F32) for i in range(2)]; LtB = [Buf(), Buf()]; Ltd = [S.dsem(), S.dsem()]
        ett = [sb(ps, "ett%d" % i, [128, H2, 1], F32) for i in range(2)]; ettB = [Buf(), Buf()]; ettd = [S.dsem(), S.dsem()]
        Sbf = [sb(ps, "Sbf%d" % i, [128, DI], BF16) for i in range(2)]; SbfB = [Buf(), Buf()]; Sbd = [S.dsem(), S.dsem()]
        Pd = sb(ps, "Pd", [128, H2], F32); PdB = Buf()
        selb = sb(ps, "selb", [128, 8], F32); selB = Buf()
        aginB = Buf(); agoutB = Buf(); agd = S.dsem()
        S.dma("sp", selb[:, :], x.sel[0:1, :].partition_broadcast(128), [], [selB], agd)
        cnt = [0]
        St3 = St[:, :].rearrange("p (h q) -> p h q", q=64)

        def step(tile, d, store, Lsrc=None, Psrc=None, track=False):
            i = cnt[0] % 2; cnt[0] += 1
            if store:
                S.copy("act", Sbf[i][:, :], St[:, :], [StB], [SbfB[i]])
                r0 = (d * NT + tile) * 128
                S.dma("act", x.Sin[r0:r0 + 128, :], Sbf[i][:, :], [SbfB[i]], [], Sbd[i])
            if Lsrc is None:
                r0 = (d * NT + tile) * 128
                Lsrc = x.Ls[r0:r0 + 128, :]
                Psrc = x.etot[tile:tile + 1, :].partition_broadcast(128)
                rd = []
            else:
                rd = [agoutB]
            S.dma("sp", Lt[i][:, :], Lsrc, rd, [LtB[i]], Ltd[i])
            S.dma("sp", ett[i][:, :, 0], Psrc, rd, [ettB[i]], ettd[i])
            S.tt("dve", St3, St3, bc(ett[i][:, d * H:(d + 1) * H, :], [128, H, 64]), ALU.mult, [ettB[i]], [StB])
            S.tt("dve", St[:, :], St[:, :], Lt[i][:, :], ALU.add, [LtB[i]], [StB])
            if track:
                S.tt("dve", Pd[:, d * H:(d + 1) * H], Pd[:, d * H:(d + 1) * H], ett[i][:, d * H:(d + 1) * H, 0], ALU.mult,
                     [ettB[i]], [PdB])

        for d in range(2):
            S.memset("pool", St[:, :], 0.0, [StB])
            for tile in (c.ownA if d == 0 else c.ownA[::-1]):
                step(tile, d, True)
        S.memset("dve", Pd[:, :], 1.0, [PdB])
        for d in range(2):
            S.memset("pool", St[:, :], 0.0, [StB])
            for tile in (c.ownB if d == 0 else c.ownB[::-1]):
                step(tile, d, False, track=True)
            S.dma("sp", x.agin[d * 128:(d + 1) * 128, :], St[:, :], [StB], [aginB], agd)
        S.dma("sp", x.agin[256:257, 0:H2], Pd[0:1, :], [PdB], [aginB], agd)
        agin, agout = x.agin, x.agout
        S.coll(lambda e: e.collective_compute("AllGather", ALU.bypass, replica_groups=[list(range(8))],
                                              ins=[agin.ap().opt()], outs=[agout.ap().opt()]),
               [aginB], [agoutB])
        for d in range(2):
            S.memset("pool", accs[:, :], 0.0, [accsB])
            for chain in ((0, 1, 2, 3), (4, 5, 6, 7)):
                S.memset("pool", St[:, :], 0.0, [StB])
                for p in (chain if d == 0 else chain[::-1]):
                    S.stt("dve", accs[:, :], St[:, :], selb[:, p:p + 1], accs[:, :], ALU.mult, ALU.add, [StB, selB], [accsB])
                    step(None, d, False, Lsrc=agout[p * 257 + d * 128:p * 257 + (d + 1) * 128, :],
                         Psrc=agout[p * 257 + 256:p * 257 + 257, 0:H2].partition_broadcast(128))
            S.copy("dve", St[:, :], accs[:, :], [accsB], [StB])
            for tile in (c.ownB if d == 0 else c.ownB[::-1]):
                step(tile, d, True)
        S.barrier()
        S.flush()

    with ExitStack() as po:
        AH, KVH = c.AH, c.KVH
        biasm = sb(po, "biasm", [128, AH, 384], F32); biasB = Buf()
        with ExitStack() as ps:
            OH = sb(ps, "OH", [128, 33, 384], F32); rb = sb(ps, "rb", [128, 32 * AH], F32); ohB = Buf(); ohd = S.dsem()
            S.dma("sp", OH[:, :, :], x.ohm[:, :].rearrange("p (b k) -> p b k", k=384), [], [ohB], ohd)
            S.dma("sp", rb[:, :], x.relb[0:1, :].partition_broadcast(128), [], [ohB], ohd)
            hB = [Buf() for _ in range(AH)]
            for h in range(AH):
                eng = "dve"
                S.copy(eng, biasm[:, h, :], OH[:, 32, :], [ohB], [hB[h]])
                for b in range(32):
                    S.stt(eng, biasm[:, h, :], OH[:, b, :], rb[:, b * AH + h:b * AH + h + 1], biasm[:, h, :],
                          ALU.mult, ALU.add, [ohB], [hB[h]])
            S.barrier()
            S.flush()
        with ExitStack() as ps:
            t = load_small(ps)
            dsk3 = sb(ps, "dsk3", [128, H, 1], F32); snwb = sb(ps, "snwb", [128, DI], F32)
            sinkb = sb(ps, "sinkb", [128, AH], F32); hvb = sb(ps, "hvb", [128, 4], F32); pen = sb(ps, "pen", [128, 4], F32)
            ds = S.dsem()
            S.dma("sp", dsk3[:, :, 0], x.dskip[0:1, :].partition_broadcast(128), [], [t.B], ds)
            S.dma("sp", snwb[:, :], x.snw[0:1, :].partition_broadcast(128), [], [t.B], ds)
            S.dma("sp", sinkb[:, :], x.sink[0:1, :].partition_broadcast(128), [], [t.B], ds)
            S.dma("sp", hvb[:, :], x.hval[0:1, :].partition_broadcast(128), [], [t.B], ds)
            S.ts("dve", pen[:, :], hvb[:, :], -1.0, -NEG, ALU.add, ALU.mult, [], [t.B])
            xs = sb(ps, "xs", [128, CT, 128], BF16); xsB = Buf(); xsd = S.dsem()
            xtok = sb(ps, "xtok", [128, DI], BF16); xtokB = Buf()
            xdt = [sb(ps, "xdt%d" % d, [128, DI], BF16) for d in range(2)]; xdtB = [Buf(), Buf()]
            Sd = [sb(ps, "Sd%d" % d, [128, DI], BF16) for d in range(2)]; SdB = [Buf(), Buf()]; Sdd = [S.dsem(), S.dsem()]
            zt = sb(ps, "zt", [128, DI], BF16); ztB = Buf(); ztd = S.dsem()
            ecs3 = sb(ps, "ecs3", [128, H2, 1], F32); ecsB = Buf()
            CBm = [sb(ps, "CBm%d" % d, [128, G, 128], BF16) for d in range(2)]; CBmB = Buf()
            R0 = [sb(ps, "R%d" % d, [128, 1024], F32) for d in range(2)]; RB0 = [Buf(), Buf()]
            R_ = [R0, R0]; RB_ = [RB0, RB0]
            E_ = [[sb(ps, "E%d" % d, [128, 8, 128], BF16) for d in range(2)] for _ in range(2)]; EB_ = [[Buf(), Buf()] for _ in range(2)]
            MT_ = E_; MTB_ = EB_
            y = sb(ps, "y", [128, DI], F32)
            zg_ = [sb(ps, "zg", [128, 512], F32) for _ in range(2)]; zsB_ = [Buf(), Buf()]
            t1_ = [sb(ps, "t1", [128, 512], F32) for _ in range(2)]; t2_ = [sb(ps, "t2", [128, 512], F32) for _ in range(2)]
            t3s = sb(ps, "t3", [128, 512], F32); t3_ = [t3s, t3s]; t3Bs = Buf()
            t1B_ = [Buf(), Buf()]; t2B_ = [Buf(), Buf()]; t3B_ = [t3Bs, t3Bs]
            ygB_ = [Buf() for _ in range(G)]
            ms3 = sb(ps, "ms3", [128, G, 1], F32); msB = Buf()
            ynb = zt; ynbB = ztB
            ynTs = sb(ps, "ynTs", [128, DIt, 128], BF16); ynTB = Buf(); ynd = S.dsem()
            qTt = sb(ps, "qTt", [128, AH, 128], BF16); kTt = sb(ps, "kTt", [128, KVH, 384], BF16)
            vt = sb(ps, "vt", [128, 3, c.KVD], BF16); qkvB = Buf(); qkd = S.dsem()
            lg_ = [sb(ps, "lg", [128, 4, 384], F32) for _ in range(2)]; lgB_ = [Buf(), Buf()]
            pe_ = lg_; peB_ = lgB_
            st_ = [sb(ps, "st4", [128, 12, 1], F32) for _ in range(2)]; stB_ = [Buf(), Buf()]
            pn_ = [sb(ps, "pn", [128, 4, 384], BF16) for _ in range(2)]; pnB_ = [Buf(), Buf()]
            PnT_ = [sb(ps, "PnT", [128, 3, 512], BF16) for _ in range(2)]; PnTB_ = [Buf(), Buf()]
            atst = sb(ps, "atst", [128, AH, 128], BF16); atB = Buf(); atd = S.dsem()
            scale = float(128 ** -0.5)
            edge = {c.ownA[0]: (0, 0), c.ownA[-1]: (1, 256), c.ownB[0]: (2, 0), c.ownB[-1]: (3, 256)}
            for tile in c.own:
                tk = slice(tile * 128, (tile + 1) * 128)
                S.dma3("sp", xs[:, :, :], x.xcsT[:, tk].rearrange("(ct p) t -> p ct t", p=128), [], [xsB], xsd)
                for d in range(2):
                    r0 = (d * NT + tile) * 128
                    S.dma("sp", Sd[d][:, :], x.Sin[r0:r0 + 128, :], [], [SdB[d]], Sdd[d])
                S.dma("sp", zt[:, :], x.zs[tk, :], [], [ztB], ztd)
                transpose_to(lambda j: xs[:, j, :], DIt,
                             lambda j0, n: xtok[:, j0 * 128:(j0 + n) * 128].rearrange("p (k t) -> p k t", t=128), xsB, xtokB)
                dt_chunk(t, tile)
                S.act(ecs3[:, :, 0], t.cs4[:, 0:H2], AF.Exp, [t.dB], [ecsB])
                for d in range(2):
                    S.tt("dve" if d == 0 else "pool", xdt[d][:, :].rearrange("p (h q) -> p h q", q=64),
                         xtok[:, :].rearrange("p (h q) -> p h q", q=64), bc(t.dtv[:, d * H:(d + 1) * H, :], [128, H, 64]),
                         ALU.mult, [xtokB, t.dB], [xdtB[d]])
                for g0 in range(0, G, 4):
                    ng = min(4, G - g0)
                    pf, pfB = bankf()
                    S.mmv([(pf[:, i * 128:(i + 1) * 128], xs[:, DIt + g0 + i, :], xs[:, DIt + G + g0 + i, :], True, True)
                           for i in range(ng)], [xsB], [pfB])
                    pv = pf[:, 0:ng * 128].rearrange("p (g l) -> p g l", l=128)
                    S.tt("dve", CBm[0][:, g0:g0 + ng, :], pv, bc(U3, [128, ng, 128]), ALU.mult, [pfB, Bc], [CBmB])
                    S.tt("dve", CBm[1][:, g0:g0 + ng, :], pv, bc(UT3, [128, ng, 128]), ALU.mult, [pfB, Bc], [CBmB])
                for g in range(G):
                    gp = g % 2
                    R, RB, E, EB, MT, MTB = R_[gp], RB_[gp], E_[gp], EB_[gp], MT_[gp], MTB_[gp]
                    t1, t2, t3, t1B, t2B, t3B = t1_[gp], t2_[gp], t3_[gp], t1B_[gp], t2B_[gp], t3B_[gp]
                    zg, zsB = zg_[gp], zsB_[gp]
                    yB = ygB_[g]
                    for d in range(2):
                        S.tt("pool", R[d][:, :].rearrange("p (k l) -> p k l", l=128), bc(t.a3[:, d * H + g * 8:d * H + g * 8 + 8, :], [128, 8, 128]),
                             bc(U3 if d == 0 else UT3, [128, 8, 128]), ALU.mult, [t.dB, Bc], [RB[d]])
                        for hf in range(2):
                            pf, pfB = bankf()
                            S.mm(pf[:, 0:512], [(SUT if d == 0 else SU, R[d][:, hf * 512:(hf + 1) * 512])], True, True,
                                 [RB[d], Bc], [pfB])
                            S.act(E[d][:, hf * 4:(hf + 1) * 4, :], pf[:, 0:512].rearrange("p (k l) -> p k l", l=128), AF.Exp,
                                  [pfB], [EB[d]])
                        S.tt("dve", MT[d][:, :, :], E[d][:, :, :], bc(CBm[d][:, g:g + 1, :], [128, 8, 128]), ALU.mult,
                             [EB[d], CBmB], [MTB[d]])
                    yi, yiB = bankf()
                    items = []
                    for k in range(8):
                        hh = g * 8 + k
                        items.append((yi[:, k * 64:(k + 1) * 64], MT[0][:, k, :], xdt[0][:, hh * 64:(hh + 1) * 64], True, False))
                        items.append((yi[:, k * 64:(k + 1) * 64], MT[1][:, k, :], xdt[1][:, hh * 64:(hh + 1) * 64], False, True))
                    S.mmv(items, [MTB[0], MTB[1], xdtB[0], xdtB[1]], [yiB])
                    ysb = []
                    for d in range(2):
                        pf, pfB = bankf()
                        S.mm(pf[:, 0:512], [(xs[:, DIt + G + g, :], Sd[d][:, g * 512:(g + 1) * 512])], True, True,
                             [xsB, SdB[d]], [pfB])
                        ysb.append((pf, pfB))
                    v3 = lambda ap: ap.rearrange("p (k q) -> p k q", q=64)
                    yg = y[:, g * 512:(g + 1) * 512]
                    S.tt("dve", v3(t1[:, :]), v3(ysb[0][0][:, 0:512]), bc(ecs3[:, g * 8:g * 8 + 8, :], [128, 8, 64]), ALU.mult,
                         [ysb[0][1], ecsB], [t1B])
                    S.tt("dve", v3(t2[:, :]), v3(ysb[1][0][:, 0:512]), bc(ecs3[:, H + g * 8:H + g * 8 + 8, :], [128, 8, 64]),
                         ALU.mult, [ysb[1][1], ecsB], [t2B])
                    S.tt("dve", yg, yi[:, 0:512], t1[:, :], ALU.add, [yiB, t1B], [yB])
                    S.tt("pool", yg, yg, t2[:, :], ALU.add, [t2B], [yB])
                    S.tt("pool", v3(t3[:, :]), v3(xtok[:, g * 512:(g + 1) * 512]), bc(dsk3[:, g * 8:g * 8 + 8, :], [128, 8, 64]),
                         ALU.mult, [xtokB, t.B], [t3B])
                    S.tt("pool", yg, yg, t3[:, :], ALU.add, [t3B], [yB])
                    S.act(zg[:, :], zt[:, g * 512:(g + 1) * 512], AF.Silu, [ztB], [zsB])
                    S.tt("dve", yg, yg, zg[:, :], ALU.mult, [zsB], [yB])
                    S.tt("pool", zg[:, :], yg, yg, ALU.mult, [yB], [zsB])
                    S.op("dve", lambda e, g=g, zg=zg: e.reduce_sum(out=ms3[:, g, :], in_=zg[:, :], axis=AX.X), [zsB], [msB])
                S.act(ms3[:, :, 0], ms3[:, :, 0], AF.Ln, [Bc], [msB], bias=x.epsb[:, 0:1], scale=1.0 / (DI // G))
                S.act(ms3[:, :, 0], ms3[:, :, 0], AF.Exp, [], [msB], scale=-0.5)
                S.tt("dve", y[:, :].rearrange("p (g q) -> p g q", g=G), y[:, :].rearrange("p (g q) -> p g q", g=G),
                     bc(ms3[:, :, :], [128, G, DI // G]), ALU.mult, [msB], ygB_)
                S.tt("pool", ynb[:, :], y[:, :], snwb[:, :], ALU.mult, ygB_ + [t.B], [ynbB])
                transpose_to(lambda j: ynb[:, j * 128:(j + 1) * 128], DIt, lambda j0, n: ynTs[:, j0:j0 + n, :], ynbB, ynTB)
                S.dma3("act", x.ynT[:, tk].rearrange("(ct p) t -> p ct t", p=128), ynTs[:, :, :], [ynTB], [], ynd)
                S.dma3("sp", qTt[:, :, :], x.qT[:, tk].rearrange("(h d) t -> d h t", d=128), [], [qkvB], qkd)
                S.dma("sp", kTt[:, :, :], x.kT[:, (tile - 1) * 128:(tile + 2) * 128].rearrange("(h d) t -> d h t", d=128),
                      [], [qkvB], qkd)
                S.dma("sp", vt[:, :, :], x.vs[(tile - 1) * 128:(tile + 2) * 128, :].rearrange("(b p) n -> p b n", p=128),
                      [], [qkvB], qkd)
                for kv in range(KVH):
                    kp = kv % 2
                    lg, lgB, pe32, peB, st4, stB4, pn, pnB, PnT, PnTB = (lg_[kp], lgB_[kp], pe_[kp], peB_[kp], st_[kp], stB_[kp],
                                                                         pn_[kp], pnB_[kp], PnT_[kp], PnTB_[kp])
                    for r in range(4):
                        h = kv * 4 + r
                        pf, pfB = bankf()
                        S.mm(pf[:, 0:384], [(qTt[:, h, :], kTt[:, kv, :])], True, True, [qkvB], [pfB])
                        S.stt("dve", lg[:, r, :], pf[:, 0:384], scale, biasm[:, h, :], ALU.mult, ALU.add, [pfB, biasB], [lgB])
                    if tile in edge:
                        pi, c0 = edge[tile]
                        S.ts("dve", lg[:, :, c0:c0 + 128], lg[:, :, c0:c0 + 128], pen[:, pi:pi + 1], None, ALU.add, None,
                             [t.B], [lgB])
                    S.op("dve", lambda e, lg=lg, st4=st4: e.reduce_max(out=st4[:, 0:4, 0], in_=lg[:, :, :], axis=AX.X), [lgB], [stB4])
                    S.tt("dve", st4[:, 0:4, 0], st4[:, 0:4, 0], sinkb[:, kv * 4:(kv + 1) * 4], ALU.max, [t.B], [stB4])
                    S.tt("dve", lg[:, :, :], lg[:, :, :], bc(st4[:, 0:4, :], [128, 4, 384]), ALU.subtract, [stB4], [lgB])
                    S.act(pe32[:, :, :], lg[:, :, :], AF.Exp, [lgB], [peB])
                    S.tt("dve", st4[:, 4:8, 0], sinkb[:, kv * 4:(kv + 1) * 4], st4[:, 0:4, 0], ALU.subtract, [t.B], [stB4])
                    S.act(st4[:, 4:8, 0], st4[:, 4:8, 0], AF.Exp, [], [stB4])
                    S.op("dve", lambda e, pe32=pe32, st4=st4: e.reduce_sum(out=st4[:, 8:12, 0], in_=pe32[:, :, :], axis=AX.X),
                         [peB], [stB4])
                    S.tt("dve", st4[:, 8:12, 0], st4[:, 8:12, 0], st4[:, 4:8, 0], ALU.add, [], [stB4])
                    S.op("dve", lambda e, st4=st4: e.reciprocal(out=st4[:, 8:12, 0], in_=st4[:, 8:12, 0]), [], [stB4])
                    S.tt("dve", pn[:, :, :], pe32[:, :, :], bc(st4[:, 8:12, :], [128, 4, 384]), ALU.mult, [peB, stB4], [pnB])
                    pa, paB = bankb()
                    S.tr([(pa[:, kb * 512 + r * 128:kb * 512 + (r + 1) * 128], pn[:, r, kb * 128:(kb + 1) * 128])
                          for kb in range(2) for r in range(4)], x.identb[:, :], [pnB, Bc], [paB])
                    S.copy("act", PnT[:, 0:2, :], pa[:, 0:1024].rearrange("p (b n) -> p b n", n=512), [paB], [PnTB])
                    pb_, pbB_ = bankb()
                    S.tr([(pb_[:, r * 128:(r + 1) * 128], pn[:, r, 256:384]) for r in range(4)], x.identb[:, :], [pnB, Bc], [pbB_])
                    S.copy("act", PnT[:, 2, :], pb_[:, 0:512], [pbB_], [PnTB])
                    pf, pfB = bankf()
                    S.mm(pf[:, 0:512], [(vt[:, kb, kv * 128:(kv + 1) * 128], PnT[:, kb, :]) for kb in range(3)], True, True,
                         [qkvB, PnTB], [pfB])
                    S.copy("act", atst[:, kv * 4:(kv + 1) * 4, :], pf[:, 0:512].rearrange("p (r q) -> p r q", q=128), [pfB], [atB])
                S.dma3("act", x.atT[:, tk].rearrange("(h d) t -> d h t", d=128), atst[:, :, :], [atB], [], atd)
            S.barrier()
            S.flush()


def build_phase_c(x):
    S, nc, c = x.S, x.nc, x.c
    sb, bankf, Bc = x.sb, x.bankf, x.Bc
    D, DI, DFF = c.D, c.DI, c.DFF
    KT = D // 128; DIt = DI // 128; FT = DFF // 128
    NA = max(DIt + KT, FT)
    WKT = x.WKT
    with ExitStack() as ps:
        W = x.WStream(ps, 5)
        A48 = sb(ps, "A48", [128, NA, 512], BF16); AB = Buf(); Ad = S.dsem()
        h1 = sb(ps, "h1", [128, 4, D], F32); hB = Buf(); hd = S.dsem(); od = S.dsem()
        gc = [sb(ps, "gc%d" % i, [128, 4, 512], BF16) for i in range(2)]; gcB = [Buf(), Buf()]; gcd = [S.dsem(), S.dsem()]
        tq = sb(ps, "tq", [128, 4, 512], F32); tqB = [Buf() for _ in range(4)]
        tq2 = sb(ps, "tq2", [128, 512], F32); tq2B = Buf()
        mg = sb(ps, "mg", [128, KT, 512], BF16); mgB = Buf()
        tmpb = sb(ps, "tmpbc", [128, 1, D], BF16); tmpB = Buf()
        ss = sb(ps, "ssc", [128, 8], F32); ssB = Buf()
        nwf = sb(ps, "nwf", [128, D], F32); nwo = sb(ps, "nwo", [128, D], F32); nwB = Buf(); nwd = S.dsem()
        S.dma("sp", nwf[:, :], x.normw[1:2, :].partition_broadcast(128), [], [nwB], nwd)
        S.dma("sp", nwo[:, :], x.normw[2:3, :].partition_broadcast(128), [], [nwB], nwd)
        sqv = tq[:, :, :].rearrange("p a b -> p (a b)")[:, 0:D]

        def stream(wsrc, Ktiles, c0, wd, mode, act_fn, actB, evac):
            nout = wd // 128 if mode == "ws" else 4
            banks = [bankf() for _ in range(nout)]
            panels = [(k0, min(WKT, Ktiles - k0)) for k0 in range(0, Ktiles, WKT)]
            for pi, (k0, nk) in enumerate(panels):
                wt, wB = W.load(wsrc, k0, nk, c0, wd)
                for u in range(nout):
                    if mode == "ws":
                        pairs = [(wt[:, j, u * 128:(u + 1) * 128], act_fn(k0 + j, None)) for j in range(nk)]
                        out = banks[u][0][:, 0:512]
                    else:
                        pairs = [(act_fn(k0 + j, u), wt[:, j, 0:wd]) for j in range(nk)]
                        out = banks[u][0][:, 0:wd]
                    S.mm(out, pairs, pi == 0, pi == len(panels) - 1, [actB, wB], [banks[u][1]])
            for u in range(nout):
                evac(u, banks[u][0], banks[u][1])

        own_blocks = [c.own[i:i + 4] for i in range(0, len(c.own), 4)]
        for bi, blk in enumerate(own_blocks):
            assert blk[3] == blk[0] + 3
            r0 = blk[0] * 128
            tk = slice(r0, r0 + 512)
            S.dma3("sp", A48[:, 0:DIt, :], x.ynT[:, tk].rearrange("(ct p) t -> p ct t", p=128), [], [AB], Ad)
            S.dma3("sp", A48[:, DIt:DIt + KT, :], x.atT[:, tk].rearrange("(ct p) t -> p ct t", p=128), [], [AB], Ad)
            S.dma("sp", h1[:, :, :], x.xin[tk, :].rearrange("(t p) d -> p t d", p=128), [], [hB], hd)
            for cc in range(D // 512):
                for gi in range(2):
                    S.dma("sp", gc[gi][:, :, :],
                          x.gT[gi * D + cc * 512:gi * D + (cc + 1) * 512, tk].rearrange("(u p) t -> p u t", p=128),
                          [], [gcB[gi]], gcd[gi])

                def ev_a(u, pf, pfB):
                    S.tt("dve", tq[:, u, :], pf[:, 0:512], gc[0][:, u, :], ALU.mult, [pfB, gcB[0]], [tqB[u]])
                stream(x.wb_ssm, DIt, cc * 512, 512, "ws", lambda k, u: A48[:, k, :], AB, ev_a)

                def ev_b(u, pf, pfB, cc=cc):
                    S.tt("dve", tq2[:, :], pf[:, 0:512], gc[1][:, u, :], ALU.mult, [pfB, gcB[1]], [tq2B])
                    S.tt("pool", mg[:, cc * 4 + u, :], tq[:, u, :], tq2[:, :], ALU.add, [tqB[u], tq2B], [mgB])
                stream(x.wb_attn, KT, cc * 512, 512, "ws", lambda k, u: A48[:, DIt + k, :], AB, ev_b)
            for cb in range(D // 512):
                def ev_o(tt, pf, pfB, cb=cb):
                    S.tt("dve", h1[:, tt, cb * 512:(cb + 1) * 512], pf[:, 0:512], h1[:, tt, cb * 512:(cb + 1) * 512], ALU.add,
                         [pfB], [hB])
                stream(x.wb_out, KT, cb * 512, 512, "as", lambda k, tt: mg[:, k, tt * 128:(tt + 1) * 128], mgB, ev_o)
            for tt in range(4):
                S.act(sqv, h1[:, tt, :], AF.Square, [hB], [tqB[0], tqB[1], tqB[2], tqB[3], ssB], scale=float(D) ** -0.5,
                      accum_out=ss[:, tt:tt + 1])
            x.rstd_ops(ss[:, 0:4], ssB)
            for tt in range(4):
                S.stt("dve", tmpb[:, 0, :], h1[:, tt, :], ss[:, tt:tt + 1], nwf[:, :], ALU.mult, ALU.mult, [hB, ssB, nwB], [tmpB])
                x.transpose_to(lambda j: tmpb[:, 0, j * 128:(j + 1) * 128], KT,
                               lambda j0, n, tt=tt: mg[:, j0:j0 + n, tt * 128:(tt + 1) * 128], tmpB, mgB)
            for fc in range(DFF // 512):
                def ev_g(u, pf, pfB):
                    S.act(tq[:, u, :], pf[:, 0:512], AF.Silu, [pfB], [tqB[u]])
                stream(x.wb_fi, KT, fc * 512, 512, "ws", lambda k, u: mg[:, k, :], mgB, ev_g)

                def ev_u(u, pf, pfB, fc=fc):
                    S.tt("dve", A48[:, fc * 4 + u, :], pf[:, 0:512], tq[:, u, :], ALU.mult, [pfB, tqB[u]], [AB])
                stream(x.wb_fi, KT, DFF + fc * 512, 512, "ws", lambda k, u: mg[:, k, :], mgB, ev_u)
            for cb in range(D // 512):
                def ev_f(tt, pf, pfB, cb=cb):
                    S.tt("dve", h1[:, tt, cb * 512:(cb + 1) * 512], pf[:, 0:512], h1[:, tt, cb * 512:(cb + 1) * 512], ALU.add,
                         [pfB], [hB])
                stream(x.wb_fo, FT, cb * 512, 512, "as", lambda k, tt: A48[:, k, tt * 128:(tt + 1) * 128], AB, ev_f)
            for tt in range(4):
                S.act(sqv, h1[:, tt, :], AF.Square, [hB], [tqB[0], tqB[1], tqB[2], tqB[3], ssB], scale=float(D) ** -0.5,
                      accum_out=ss[:, 4 + tt:5 + tt])
            x.rstd_ops(ss[:, 4:8], ssB)
            for tt in range(4):
                S.stt("dve", h1[:, tt, :], h1[:, tt, :], ss[:, 4 + tt:5 + tt], nwo[:, :], ALU.mult, ALU.mult, [ssB, nwB], [hB])
            o0 = bi * 512
            S.dma("act", x.y_out[o0:o0 + 512, :].rearrange("(t p) d -> p t d", p=128), h1[:, :, :], [hB], [], od)


def _buckets(rel):
    half = 16
    ret = (rel > 0).astype(np.int32) * half
    n = np.abs(rel)
    me = half // 2
    lg = me + (np.log(np.maximum(n, 1) / me) / np.log(128 / me) * (half - me)).astype(np.int32)
    lg = np.minimum(lg, half - 1)
    return ret + np.where(n < me, n, lg).astype(np.int32)


def host_consts():
    i = np.arange(128)
    ident = (i[:, None] == i[None, :])
    Um = (i[:, None] <= i[None, :])
    UTm = (i[:, None] >= i[None, :])
    SUm = (i[:, None] < i[None, :])
    SUTm = (i[:, None] > i[None, :])
    ones = np.ones((128, 128), bool)
    cst = np.concatenate([ident, Um, UTm, SUm, SUTm, ones], axis=1).astype(np.float32)
    rel = np.arange(384)[None, :] - 128 - np.arange(128)[:, None]
    bk = _buckets(rel)
    oh = np.zeros((128, 33, 384), np.float32)
    for b in range(32):
        oh[:, b, :] = (bk == b)
    oh[:, 32, :] = np.where(np.abs(rel) <= 128, 0.0, NEG)
    return cst, oh.reshape(128, 33 * 384)


def make_in_maps(cfg, inp):
    c = cfg
    cst, ohm = host_consts()
    f = lambda a: np.ascontiguousarray(a, dtype=np.float32)
    common = {
        "w_in": f(inp["w_in"][0]), "w_ssm": f(inp["w_ssm_branch"][0]), "w_attn": f(inp["w_attn_branch"][0]),
        "w_out": f(inp["w_out"][0]), "w_fi": f(inp["w_ffn_in"][0]), "w_fo": f(inp["w_ffn_out"][0]),
        "normw": f(np.stack([inp["mix_norm_w"][0], inp["ffn_norm_w"][0], inp["final_norm_w"]])),
        "conv_w": f(inp["conv_w"][0].reshape(5, c.CT, 128).transpose(2, 1, 0).reshape(128, c.CT * 5)),
        "conv_b": f(inp["conv_b"][0].reshape(c.CT, 128).T),
        "dtb": f(inp["dt_bias"][0].reshape(1, -1)), "alog": f(inp["a_log"][0].reshape(1, -1)),
        "dskip": f(inp["d_skip"][0].reshape(1, -1)), "snw": f(inp["ssm_norm_w"][0].reshape(1, -1)),
        "relb": f(inp["rel_bias"].reshape(1, -1)), "sink": f(inp["attn_sink"][0].reshape(1, -1)),
        "cst": cst, "ohm": ohm,
    }
    xp, xs = inp["x_prompt"], inp["x_sample"]
    maps = []
    for core in range(8):
        s, qd = core // 4, core % 4
        xin = np.zeros((c.NTOK, c.D), np.float32)
        xin[128:128 + c.SEQ] = xp[core]
        b0 = (c.TA + 2) * 128
        lo = qd * c.QL - 128
        hi = (qd + 1) * c.QL + 128
        slo, shi = max(lo, 0), min(hi, c.DEC_SEQ)
        xin[b0 + (slo - lo): b0 + (slo - lo) + (shi - slo)] = xs[s, slo:shi]
        sel = np.zeros((1, 8), np.float32); sel[0, core] = 1.0
        hval = np.array([[0.0, 0.0, 1.0 if qd > 0 else 0.0, 1.0 if qd < 3 else 0.0]], np.float32)
        m = dict(common)
        m.update({"xin": xin, "sel": sel, "hval": hval})
        maps.append(m)
    return maps


def run(cfg, inp):
    nc = build(cfg)
    maps = make_in_maps(cfg, inp)
    res = run_bass_kernel_spmd(nc, maps, core_ids=list(range(8)))
    c = cfg
    yp = np.zeros((8, c.SEQ, c.D), np.float32)
    ys = np.zeros((2, c.DEC_SEQ, c.D), np.float32)
    for core in range(8):
        y = res.results[core]["y_out"]
        yp[core] = y[:c.SEQ]
        s, qd = core // 4, core % 4
        ys[s, qd * c.QL:(qd + 1) * c.QL] = y[c.SEQ:]
    return (yp, ys), res


def kernel(**inputs):
    out, _ = run(CFG_FULL, inputs)
    return out
```

```python
import math
from contextlib import ExitStack
from types import SimpleNamespace
import numpy as np
import concourse.bass as bass
import concourse.mybir as mybir
from concourse.bass_utils import run_bass_kernel_spmd

F32 = mybir.dt.float32
BF16 = mybir.dt.bfloat16
AF = mybir.ActivationFunctionType
ALU = mybir.AluOpType
AX = mybir.AxisListType
EPS = 1e-6
NEG = -30000.0
P = 128


def make_cfg(D, SEQ, DEC_SEQ, G):
    c = SimpleNamespace()
    c.D = D; c.DI = 2 * D; c.H = c.DI // 64; c.G = G; c.K = c.H // G
    assert c.K == 8
    c.GN = G * 128; c.CD = c.DI + 2 * c.GN; c.CT = c.CD // 128
    c.AH = D // 128; c.KVH = c.AH // 4; c.QD = D; c.KVD = c.KVH * 128
    c.DFF = -(-8 * D // (3 * 256)) * 256
    c.H2 = 2 * c.H
    c.IN = c.DI + c.CD + c.H2 + c.QD + 2 * c.KVD + 2 * D
    c.SEQ = SEQ; c.DEC_SEQ = DEC_SEQ; c.QL = DEC_SEQ // 4
    c.TA = SEQ // 128; c.TB = c.QL // 128
    assert c.TA % 4 == 0 and c.TB % 4 == 0
    c.NT = c.TA + c.TB + 4; c.NTOK = c.NT * 128
    c.ownA = list(range(1, c.TA + 1)); c.ownB = list(range(c.TA + 3, c.TA + c.TB + 3))
    c.own = c.ownA + c.ownB
    c.oz = 0; c.oxbc = c.DI; c.odt = c.DI + c.CD; c.oq = c.odt + c.H2
    c.ok = c.oq + c.QD; c.ov = c.ok + c.KVD; c.og = c.ov + c.KVD
    c.debug = False
    return c


CFG_FULL = make_cfg(2048, 2048, 16384, 8)


class Buf:
    __slots__ = ("w", "r")

    def __init__(self):
        self.w = {}
        self.r = {}


class Sched:
    ENGS = ("pe", "act", "dve", "pool", "sp")

    def __init__(self, nc, es):
        self.nc = nc
        self.es = es
        self.q = {e: [] for e in self.ENGS}
        self.sem = {}
        self.tot = {}
        self.seen = {e: {} for e in self.ENGS}
        for e in self.ENGS:
            self.sem[e] = es.enter_context(nc.semaphore("s_" + e))
            self.tot[e] = 0
        self.nd = 0

    def dsem(self):
        k = "d%d" % self.nd
        self.nd += 1
        self.sem[k] = self.es.enter_context(self.nc.semaphore("s_" + k))
        self.tot[k] = 0
        return k

    def _waits(self, eng, reads, writes):
        need = {}
        for b in reads:
            for k, v in b.w.items():
                if need.get(k, 0) < v:
                    need[k] = v
        for b in writes:
            for dct in (b.w, b.r):
                for k, v in dct.items():
                    if need.get(k, 0) < v:
                        need[k] = v
        out = []
        seen = self.seen[eng]
        for k, v in need.items():
            if k == "pe" and eng == "pe":
                continue
            if k[0] == "d":
                v = self.tot[k]
            if seen.get(k, 0) < v:
                seen[k] = v
                out.append((k, v))
        return out

    def _mark(self, key, val, reads, writes):
        for b in writes:
            b.w = {key: val}
            b.r = {}
        for b in reads:
            if b.r.get(key, 0) < val:
                b.r[key] = val

    def op(self, eng, fn, reads=(), writes=()):
        waits = self._waits(eng, reads, writes)
        self.tot[eng] += 1
        self.q[eng].append((waits, fn, eng, 1))
        self._mark(eng, self.tot[eng], reads, writes)

    def dma(self, eng, out, in_, reads, writes, ds, slow=False):
        waits = self._waits(eng, reads, writes)
        self.tot[ds] += 16
        if slow:
            fn = lambda e: e.dma_start(out=out, in_=in_, allow_slow_non_contiguous=True)
        else:
            fn = lambda e: e.dma_start(out=out, in_=in_)
        self.q[eng].append((waits, fn, ds, 16))
        self._mark(ds, self.tot[ds], reads, writes)

    def mm(self, out, pairs, first, last, reads, writes):
        pairs = list(pairs)
        n = len(pairs)

        def fn(e):
            ins = None
            for i, (l, r) in enumerate(pairs):
                ins = e.matmul(out, l, r, start=(first and i == 0), stop=(last and i == n - 1))
            return ins
        self.op("pe", fn, reads, writes)

    def mmv(self, items, reads, writes):
        items = list(items)

        def fn(e):
            ins = None
            for (o, l, r, st, sp) in items:
                ins = e.matmul(o, l, r, start=st, stop=sp)
            return ins
        self.op("pe", fn, reads, writes)

    def tr(self, items, ident, reads, writes):
        items = list(items)

        def fn(e):
            ins = None
            for (o, i) in items:
                ins = e.transpose(out=o, in_=i, identity=ident)
            return ins
        self.op("pe", fn, reads, writes)

    def act(self, out, in_, func, reads, writes, **kw):
        self.op("act", lambda e: e.activation(out=out, in_=in_, func=func, **kw), reads, writes)

    def copy(self, eng, out, in_, reads, writes):
        if eng == "act":
            self.op("act", lambda e: e.copy(out=out, in_=in_), reads, writes)
        else:
            self.op(eng, lambda e: e.tensor_copy(out=out, in_=in_), reads, writes)

    def tt(self, eng, out, in0, in1, op, reads, writes):
        self.op(eng, lambda e: e.tensor_tensor(out=out, in0=in0, in1=in1, op=op), reads, writes)

    def ts(self, eng, out, in0, s1, s2, op0, op1, reads, writes):
        if s2 is None:
            self.op(eng, lambda e: e.tensor_scalar(out=out, in0=in0, scalar1=s1, scalar2=0.0, op0=op0, op1=ALU.add), reads, writes)
        else:
            self.op(eng, lambda e: e.tensor_scalar(out=out, in0=in0, scalar1=s1, scalar2=s2, op0=op0, op1=op1), reads, writes)

    def stt(self, eng, out, in0, scalar, in1, op0, op1, reads, writes):
        self.op(eng, lambda e: e.scalar_tensor_tensor(out=out, in0=in0, scalar=scalar, in1=in1, op0=op0, op1=op1),
                reads, writes)

    def memset(self, eng, ap, val, writes):
        self.op(eng, lambda e: e.memset(ap, val), [], writes)

    def coll(self, fn, reads, writes):
        k = "c%d" % self.nd
        self.nd += 1
        self.sem[k] = self.es.enter_context(self.nc.semaphore("s_" + k))
        self.tot[k] = 1
        waits = self._waits("pool", reads, writes)
        self.q["pool"].append((waits, fn, k, 1))
        self._mark(k, 1, reads, writes)

    def dma3(self, eng, out, in_, reads, writes, ds, step=8):
        n = out.shape[1]
        for j in range(0, n, step):
            k = min(step, n - j)
            self.dma(eng, out[:, j:j + k, :], in_[:, j:j + k, :], reads, writes, ds)

    def barrier(self):
        for e in self.ENGS:
            waits = []
            for k, v in self.tot.items():
                if k == e or v == 0:
                    continue
                if self.seen[e].get(k, 0) < v:
                    self.seen[e][k] = v
                    waits.append((k, v))
            if waits:
                self.q[e].append((waits, None, None, 0))

    def flush(self):
        nc = self.nc
        engobj = {"pe": "tensor", "act": "scalar", "dve": "vector", "pool": "gpsimd", "sp": "sync"}
        with nc.Block() as block:
            for e in self.ENGS:
                items = self.q[e]
                if not items:
                    continue

                def body(eng, items=items):
                    for waits, fn, key, inc in items:
                        for k, v in waits:
                            eng.wait_ge(self.sem[k], v)
                        if fn is not None:
                            ins = fn(eng)
                            ins.then_inc(self.sem[key], inc)

                getattr(block, engobj[e])(body)
        self.q = {e: [] for e in self.ENGS}


def bc(ap, shape):
    return ap.to_broadcast(list(shape))


def build(cfg):
    c = cfg
    nc = bass.Bass("TRN2", target_bir_lowering=False)
    D, DI, H, H2, G, CT, CD = c.D, c.DI, c.H, c.H2, c.G, c.CT, c.CD
    NT, NTOK = c.NT, c.NTOK
    KT = D // 128

    def din(name, shape, dt=F32):
        return nc.dram_tensor(name, list(shape), dt, kind="ExternalInput")

    skind = "ExternalOutput" if c.debug else "Internal"

    def dscr(name, shape, dt):
        if c.debug:
            return nc.dram_tensor(name, list(shape), dt, kind="ExternalOutput")
        return nc.dram_tensor(name, list(shape), dt)

    xin = din("xin", [NTOK, D])
    w_in = din("w_in", [D, c.IN]); w_ssm = din("w_ssm", [DI, D]); w_attn = din("w_attn", [D, D])
    w_out = din("w_out", [D, D]); w_fi = din("w_fi", [D, 2 * c.DFF]); w_fo = din("w_fo", [c.DFF, D])
    normw = din("normw", [3, D])
    conv_w = din("conv_w", [128, CT * 5]); conv_b = din("conv_b", [128, CT])
    dtb = din("dtb", [1, H2]); alog = din("alog", [1, H2]); dskip = din("dskip", [1, H])
    snw = din("snw", [1, DI]); relb = din("relb", [1, 32 * c.AH]); sink = din("sink", [1, c.AH])
    sel = din("sel", [1, 8]); hval = din("hval", [1, 4])
    cst = din("cst", [128, 6 * 128])
    ohm = din("ohm", [128, 33 * 384])
    y_out = nc.dram_tensor("y_out", [len(c.own) * 128, D], F32, kind="ExternalOutput")

    wb_in = nc.dram_tensor("wb_in", [D, c.IN], BF16); wb_ssm = nc.dram_tensor("wb_ssm", [DI, D], BF16)
    wb_attn = nc.dram_tensor("wb_attn", [D, D], BF16); wb_out = nc.dram_tensor("wb_out", [D, D], BF16)
    wb_fi = nc.dram_tensor("wb_fi", [D, 2 * c.DFF], BF16); wb_fo = nc.dram_tensor("wb_fo", [c.DFF, D], BF16)
    zs = dscr("zs", [NTOK, DI], BF16); vs = dscr("vs", [NTOK, c.KVD], BF16); dts = dscr("dts", [NTOK, H2], F32)
    xbcT = dscr("xbcT", [CD, NTOK], F32); qT = dscr("qT", [c.QD, NTOK], BF16); kT = dscr("kT", [c.KVD, NTOK], BF16)
    gT = dscr("gT", [2 * D, NTOK], BF16)
    xcsT = dscr("xcsT", [CD, NTOK], BF16)
    Ls = dscr("Ls", [2 * NT * 128, DI], F32); etot = dscr("etot", [NT, H2], F32)
    Sin = dscr("Sin", [2 * NT * 128, DI], BF16)
    agin = nc.dram_tensor("agin", [257, DI], F32); agout = nc.dram_tensor("agout", [8 * 257, DI], F32)
    ynT = dscr("ynT", [DI, NTOK], BF16); atT = dscr("atT", [c.QD, NTOK], BF16)

    es = ExitStack()
    S = Sched(nc, es)

    uid = [0]

    def sb(ps, name, shape, dt):
        uid[0] += 1
        return ps.enter_context(nc.sbuf_tensor("%s_%d" % (name, uid[0]), list(shape), dt))

    with ExitStack() as ps:
        NBUF = 3
        CW = 4096
        ld = [sb(ps, "ld%d" % i, [128, CW], F32) for i in range(NBUF)]; ldB = [Buf() for _ in range(NBUF)]
        cv = [sb(ps, "cv%d" % i, [128, CW], BF16) for i in range(NBUF)]; cvB = [Buf() for _ in range(NBUF)]
        lds = [S.dsem() for _ in range(NBUF)]; cvs = [S.dsem() for _ in range(NBUF)]
        n = 0
        for src, dst, rows, cols in ((w_in, wb_in, D, c.IN), (w_ssm, wb_ssm, DI, D), (w_attn, wb_attn, D, D),
                                     (w_out, wb_out, D, D), (w_fi, wb_fi, D, 2 * c.DFF), (w_fo, wb_fo, c.DFF, D)):
            for r0 in range(0, rows, 128):
                for c0 in range(0, cols, CW):
                    cw_ = min(CW, cols - c0)
                    i = n % NBUF
                    S.dma("sp", ld[i][:, 0:cw_], src[r0:r0 + 128, c0:c0 + cw_], [], [ldB[i]], lds[i])
                    S.copy(("act", "dve", "pool")[n % 3], cv[i][:, 0:cw_], ld[i][:, 0:cw_], [ldB[i]], [cvB[i]])
                    S.dma("act", dst[r0:r0 + 128, c0:c0 + cw_], cv[i][:, 0:cw_], [cvB[i]], [], cvs[i])
                    n += 1
        S.barrier()
        S.flush()

    gs = ExitStack()
    csb = sb(gs, "csb", [128, 768], F32)
    identb = sb(gs, "identb", [128, 128], BF16)
    psf = [gs.enter_context(nc.psum_tensor("psf%d" % i, [128, 512], F32)) for i in range(8)]
    psfB = [Buf() for _ in psf]
    psbv = [p[:, :].bitcast(BF16) for p in psf]
    rr = {"f": 0}

    def bankf():
        i = rr["f"] % len(psf); rr["f"] += 1
        return psf[i], psfB[i]

    def bankb():
        i = rr["f"] % len(psf); rr["f"] += 1
        return psbv[i], psfB[i]

    ident = csb[:, 0:128]; U = csb[:, 128:256]; UT = csb[:, 256:384]
    SU = csb[:, 384:512]; SUT = csb[:, 512:640]; ones = csb[:, 640:768]
    Bc = Buf()
    dc = S.dsem()
    S.dma("sp", csb[:, :], cst[:, :], [], [Bc], dc)
    S.op("dve", lambda e: e.tensor_copy(out=identb[:, :], in_=ident), [Bc], [Bc])
    epsb = sb(gs, "epsb", [128, 1], F32)
    S.op("pool", lambda e: e.memset(epsb[:, :], EPS), [], [Bc])

    WKT = 8

    class WStream:
        def __init__(self, ps, nslots=6):
            self.t = [sb(ps, "wp%d" % i, [128, WKT, 512], BF16) for i in range(nslots)]
            self.b = [Buf() for _ in range(nslots)]
            self.ds = [S.dsem() for _ in range(nslots)]
            self.i = 0

        def load(self, wsrc, k0, nk, c0, wd):
            i = self.i % len(self.t); self.i += 1
            src = wsrc[k0 * 128:(k0 + nk) * 128, c0:c0 + wd].rearrange("(kt p) n -> p kt n", p=128)
            S.dma("sp", self.t[i][:, 0:nk, 0:wd], src, [], [self.b[i]], self.ds[i])
            return self.t[i], self.b[i]

    def rstd_ops(ap, B, eps=EPS):
        S.act(ap, ap, AF.Ln, [Bc], [B], bias=epsb[:, 0:1], scale=1.0)
        S.act(ap, ap, AF.Exp, [], [B], scale=-0.5)

    def transpose_to(src2d_fn, nblk, dst_fn, srcB, dstB):
        for j0 in range(0, nblk, 8):
            n = min(8, nblk - j0)
            pb, pbB = bankb()
            S.tr([(pb[:, i * 128:(i + 1) * 128], src2d_fn(j0 + i)) for i in range(n)], identb[:, :], [srcB, Bc], [pbB])
            eng = "act" if (j0 // 8) % 2 == 0 else "dve"
            S.copy(eng, dst_fn(j0, n), pb[:, 0:n * 128].rearrange("p (k t) -> p k t", t=128), [pbB], [dstB])

    def rmsnorm_T(ps_tiles, src, srcB, nw, dstT, dstB, tmpb, tmpB, ss, ssB, ntile):
        sq, sqB = ps_tiles
        for tt in range(ntile):
            S.act(sq[:, :], src[:, tt, :], AF.Square, [srcB], [sqB, ssB], scale=float(D) ** -0.5,
                  accum_out=ss[:, tt:tt + 1])
        rstd_ops(ss[:, 0:ntile], ssB)
        for tt in range(ntile):
            S.stt("dve", tmpb[:, tt, :], src[:, tt, :], ss[:, tt:tt + 1], nw[0], ALU.mult, ALU.mult,
                  [srcB, ssB, nw[1]], [tmpB])
        for tt in range(ntile):
            transpose_to(lambda j, tt=tt: tmpb[:, tt, j * 128:(j + 1) * 128], KT,
                         lambda j0, n, tt=tt: dstT[:, j0:j0 + n, tt * 128:(tt + 1) * 128], tmpB, dstB)

    NB = 4
    with ExitStack() as ps:
        xt = [sb(ps, "xt%d" % i, [128, NB, D], F32) for i in range(2)]
        xtB = [Buf(), Buf()]; xds = [S.dsem(), S.dsem()]
        sq = sb(ps, "sq", [128, D], F32); sqB = Buf()
        ss = sb(ps, "ss", [128, 8], F32); ssB = Buf()
        tmpb = sb(ps, "tmpb", [128, NB, D], BF16); tmpB = Buf()
        xnT = sb(ps, "xnT", [128, KT, NB * 128], BF16); xnB = Buf()
        stf = [sb(ps, "stf%d" % i, [128, 4, 512], F32) for i in range(2)]
        stb = [sb(ps, "stb%d" % i, [128, 4, 512], BF16) for i in range(2)]
        stfB = [Buf(), Buf()]; stbB = [Buf(), Buf()]
        sds = [S.dsem() for _ in range(4)]
        W = WStream(ps, 8)
        nwa = sb(ps, "nwa", [128, D], F32); nwaB = Buf(); nwd = S.dsem()
        S.dma("sp", nwa[:, :], normw[0:1, :].partition_broadcast(128), [], [nwaB], nwd)
        chunks = []
        for j in range(0, DI, 512):
            chunks.append((c.oz + j, 512, "tok", zs, j, "b"))
        for j in range(0, CD, 512):
            chunks.append((c.oxbc + j, 512, "feat", xbcT, j, "f"))
        chunks.append((c.odt, H2, "tok", dts, 0, "f"))
        for j in range(0, c.QD, 512):
            chunks.append((c.oq + j, 512, "feat", qT, j, "b"))
        for j in range(0, c.KVD, 512):
            wd = min(512, c.KVD - j)
            chunks.append((c.ok + j, wd, "feat", kT, j, "b"))
        for j in range(0, c.KVD, 512):
            wd = min(512, c.KVD - j)
            chunks.append((c.ov + j, wd, "tok", vs, j, "b"))
        for j in range(0, 2 * D, 512):
            chunks.append((c.og + j, 512, "feat", gT, j, "g"))
        nst = [0, 0]
        nblk = NT // NB
        for b in range(nblk):
            xi = b % 2
            S.dma("sp", xt[xi][:, :, :], xin[b * NB * 128:(b + 1) * NB * 128, :].rearrange("(t p) d -> p t d", p=128),
                  [], [xtB[xi]], xds[xi])
            rmsnorm_T((sq, sqB), xt[xi], xtB[xi], (nwa[:, :], nwaB), xnT, xnB, tmpb, tmpB, ss, ssB, NB)
            for (c0, wd, mode, dst, doff, kind) in chunks:
                panels = []
                for k0 in range(0, KT, WKT):
                    nk = min(WKT, KT - k0)
                    panels.append((k0, nk) + W.load(wb_in, k0, nk, c0, wd))
                if kind == "f":
                    si = nst[0] % 2; nst[0] += 1
                    st, stB, sd = stf[si], stfB[si], sds[si]
                else:
                    si = nst[1] % 2; nst[1] += 1
                    st, stB, sd = stb[si], stbB[si], sds[2 + si]
                nsub = 4 if mode == "tok" else wd // 128
                ncol = wd if mode == "tok" else NB * 128
                for u in range(nsub):
                    pf, pfB = bankf()
                    for pi, (k0, nk, wt, wB) in enumerate(panels):
                        if mode == "tok":
                            pairs = [(xnT[:, k0 + j, u * 128:(u + 1) * 128], wt[:, j, 0:wd]) for j in range(nk)]
                        else:
                            pairs = [(wt[:, j, u * 128:(u + 1) * 128], xnT[:, k0 + j, :]) for j in range(nk)]
                        S.mm(pf[:, 0:ncol], pairs, pi == 0, pi == len(panels) - 1, [xnB, wB], [pfB])
                    if kind == "g":
                        S.act(st[:, u, 0:ncol], pf[:, 0:ncol], AF.Sigmoid, [pfB], [stB])
                    else:
                        S.copy("act" if u % 2 == 0 else "dve", st[:, u, 0:ncol], pf[:, 0:ncol], [pfB], [stB])
                t0 = b * NB * 128
                if mode == "tok":
                    S.dma("act", dst[t0:t0 + NB * 128, doff:doff + wd].rearrange("(t p) n -> p t n", p=128),
                          st[:, 0:4, 0:wd], [stB], [], sd)
                else:
                    S.dma("act", dst[doff:doff + wd, t0:t0 + NB * 128].rearrange("(u p) t -> p u t", p=128),
                          st[:, 0:nsub, 0:NB * 128], [stB], [], sd)
        S.barrier()
        S.flush()

    ctx = SimpleNamespace(**locals())
    build_ssd_attn(ctx)
    build_phase_c(ctx)
    S.barrier()
    S.flush()
    gs.close()
    es.close()
    return nc


def build_ssd_attn(x):
    S, nc, c = x.S, x.nc, x.c
    sb, bankf, bankb, Bc, transpose_to = x.sb, x.bankf, x.bankb, x.Bc, x.transpose_to
    D, DI, H, H2, G, CT, CD, NT = c.D, c.DI, c.H, c.H2, c.G, c.CT, c.CD, c.NT
    DIt = DI // 128
    U, UT, SU, SUT, ones = x.U, x.UT, x.SU, x.SUT, x.ones
    U3 = x.csb[:, 128:256].rearrange("p (o l) -> p o l", o=1)
    UT3 = x.csb[:, 256:384].rearrange("p (o l) -> p o l", o=1)

    def load_small(ps):
        t = SimpleNamespace()
        t.B = Buf()
        ds = S.dsem()
        t.dtb = sb(ps, "dtbb", [128, H2], F32); t.A = sb(ps, "Abc", [128, H2], F32)
        t.oneb = sb(ps, "oneb", [128, 1], F32)
        S.dma("sp", t.dtb[:, :], x.dtb[0:1, :].partition_broadcast(128), [], [t.B], ds)
        S.dma("sp", t.A[:, :], x.alog[0:1, :].partition_broadcast(128), [], [t.B], ds)
        S.act(t.A[:, :], t.A[:, :], AF.Exp, [], [t.B])
        S.ts("dve", t.A[:, :], t.A[:, :], -1.0, None, ALU.mult, None, [], [t.B])
        S.memset("pool", t.oneb[:, :], 1.0, [t.B])
        t.dtr = sb(ps, "dtr", [128, H2], F32); t.vv = sb(ps, "vv", [128, H2], F32); t.ab = sb(ps, "ab", [128, H2], F32)
        t.dtv = sb(ps, "dtv", [128, H2, 1], F32); t.a3 = sb(ps, "a3", [128, H2, 1], F32)
        t.cs4 = sb(ps, "cs4", [128, 2 * H2], F32)
        t.dB = Buf(); t.dds = S.dsem()
        return t

    def dt_chunk(t, tile):
        S.dma("sp", t.dtr[:, :], x.dts[tile * 128:(tile + 1) * 128, :], [], [t.dB], t.dds)
        S.tt("dve", t.vv[:, :], t.dtr[:, :], t.dtb[:, :], ALU.add, [t.B], [t.dB])
        S.ts("dve", t.ab[:, :], t.vv[:, :], -1.0, None, ALU.mult, None, [], [t.dB])
        S.tt("dve", t.ab[:, :], t.ab[:, :], t.vv[:, :], ALU.max, [], [t.dB])
        S.act(t.ab[:, :], t.ab[:, :], AF.Exp, [], [t.dB], scale=-1.0)
        S.act(t.ab[:, :], t.ab[:, :], AF.Ln, [t.B], [t.dB], bias=t.oneb[:, 0:1], scale=1.0)
        S.ts("dve", t.vv[:, :], t.vv[:, :], 0.0, None, ALU.max, None, [], [t.dB])
        S.tt("dve", t.dtv[:, :, 0], t.vv[:, :], t.ab[:, :], ALU.add, [], [t.dB])
        S.tt("dve", t.a3[:, :, 0], t.dtv[:, :, 0], t.A[:, :], ALU.mult, [t.B], [t.dB])
        pf, pfB = bankf()
        S.mmv([(pf[:, 0:H], U, t.a3[:, 0:H, 0], True, True), (pf[:, H:H2], UT, t.a3[:, H:H2, 0], True, True),
               (pf[:, H2:2 * H2], ones, t.a3[:, :, 0], True, True)], [t.dB, Bc], [pfB])
        S.copy("dve", t.cs4[:, :], pf[:, 0:2 * H2], [pfB], [t.dB])

    with ExitStack() as ps:
        t = load_small(ps)
        cw = sb(ps, "cw", [128, CT, 5], F32); cb = sb(ps, "cb", [128, CT, 1], F32)
        ds = S.dsem()
        S.dma("sp", cw[:, :, :], x.conv_w[:, :].rearrange("p (ct j) -> p ct j", j=5), [], [t.B], ds)
        S.dma("sp", cb[:, :, 0], x.conv_b[:, :], [], [t.B], ds)
        xcs_ = [sb(ps, "xc%d" % i, [128, CT, 132], F32) for i in range(2)]; xcBs = [Buf(), Buf()]; xcds = [S.dsem(), S.dsem()]
        accs_ = [sb(ps, "acc", [128, CT, 128], F32) for _ in range(2)]; tmpc = sb(ps, "tmpc", [128, CT, 128], F32)
        CA = (CT * 2) // 3
        accBs_ = [[Buf(), Buf()], [Buf(), Buf()]]; tmpcBs = [Buf(), Buf()]
        xss = [sb(ps, "xs%d" % i, [128, CT, 128], BF16) for i in range(2)]; xsBs = [Buf(), Buf()]; xsds = [S.dsem(), S.dsem()]
        xtok = sb(ps, "xtok", [128, DI], BF16); xtokB = Buf()
        Btok = sb(ps, "Btok", [128, G, 128], BF16); BtokB = Buf()
        xw = [sb(ps, "xw%d" % d, [128, DI], BF16) for d in range(2)]; xwB = [Buf(), Buf()]
        Lst0 = sb(ps, "Lst", [128, DI], F32); Lst = [Lst0, Lst0]; LstB0 = Buf(); LstB = [LstB0, LstB0]; Lds0 = S.dsem(); Lds = [Lds0, Lds0]
        wt_ = sb(ps, "wtmp", [128, H2], F32); dw3 = sb(ps, "dw3", [128, H2, 1], F32); et = sb(ps, "et", [128, H2], F32)
        wB = Buf(); etB = Buf(); etd = S.dsem()
        def conv(ti, tile):
            xc = xcs_[ti % 2]; xcB = xcBs[ti % 2]; xs = xss[ti % 2]; xsB = xsBs[ti % 2]; acc = accs_[ti % 2]; accBs = accBs_[ti % 2]
            S.dma3("sp", xc[:, :, :], x.xbcT[:, tile * 128 - 2:tile * 128 + 130].rearrange("(ct p) t -> p ct t", p=128),
                   [], [xcB], xcds[ti % 2])
            for half, (eng, c0_, c1_) in enumerate((("dve", 0, CA), ("pool", CA, CT))):
                shp = [128, c1_ - c0_, 128]
                aB = accBs[half]; tB_ = tmpcBs[half]
                S.tt(eng, acc[:, c0_:c1_, :], xc[:, c0_:c1_, 0:128], bc(cw[:, c0_:c1_, 0:1], shp), ALU.mult, [xcB, t.B], [aB])
                for j in range(1, 5):
                    S.tt(eng, tmpc[:, c0_:c1_, :], xc[:, c0_:c1_, j:j + 128], bc(cw[:, c0_:c1_, j:j + 1], shp), ALU.mult,
                         [xcB, t.B], [tB_])
                    S.tt(eng, acc[:, c0_:c1_, :], acc[:, c0_:c1_, :], tmpc[:, c0_:c1_, :], ALU.add, [tB_], [aB])
                S.tt(eng, acc[:, c0_:c1_, :], acc[:, c0_:c1_, :], bc(cb[:, c0_:c1_, :], shp), ALU.add, [t.B], [aB])
            S.act(xs[:, :, :], acc[:, :, :], AF.Silu, accBs, [xsB])
            S.dma3("act", x.xcsT[:, tile * 128:(tile + 1) * 128].rearrange("(ct p) t -> p ct t", p=128), xs[:, :, :],
                   [xsB], [], xsds[ti % 2])
        def post(ti, tile):
            xs = xss[ti % 2]; xsB = xsBs[ti % 2]
            transpose_to(lambda j: xs[:, j, :], DIt,
                         lambda j0, n: xtok[:, j0 * 128:(j0 + n) * 128].rearrange("p (k t) -> p k t", t=128), xsB, xtokB)
            transpose_to(lambda g: xs[:, DIt + g, :], G, lambda j0, n: Btok[:, j0:j0 + n, :], xsB, BtokB)
            dt_chunk(t, tile)
            S.tt("dve", wt_[:, :], t.cs4[:, H2:2 * H2], t.cs4[:, 0:H2], ALU.subtract, [t.dB], [wB])
            S.act(wt_[:, :], wt_[:, :], AF.Exp, [], [wB])
            S.tt("dve", dw3[:, :, 0], wt_[:, :], t.dtv[:, :, 0], ALU.mult, [t.dB], [wB])
            S.act(et[:, :], t.cs4[:, H2:2 * H2], AF.Exp, [t.dB], [etB])
            S.dma("act", x.etot[tile:tile + 1, :], et[0:1, :], [etB], [], etd)
            for d in range(2):
                S.tt("dve" if d == 0 else "pool", xw[d][:, :].rearrange("p (h q) -> p h q", q=64),
                     xtok[:, :].rearrange("p (h q) -> p h q", q=64), bc(dw3[:, d * H:(d + 1) * H, :], [128, H, 64]),
                     ALU.mult, [xtokB, wB], [xwB[d]])
                for g in range(G):
                    pf, pfB = bankf()
                    S.mm(pf[:, 0:512], [(Btok[:, g, :], xw[d][:, g * 512:(g + 1) * 512])], True, True, [BtokB, xwB[d]], [pfB])
                    S.copy("act" if g % 2 == 0 else "dve", Lst[d][:, g * 512:(g + 1) * 512], pf[:, 0:512], [pfB], [LstB[d]])
                r0 = (d * NT + tile) * 128
                S.dma("act", x.Ls[r0:r0 + 128, :], Lst[d][:, :], [LstB[d]], [], Lds[d])
        own = list(c.own)
        conv(0, own[0])
        for ti, tile in enumerate(own):
            if ti + 1 < len(own):
                conv(ti + 1, own[ti + 1])
            post(ti, tile)
        S.barrier()
        S.flush()

    with ExitStack() as ps:
        St = sb(ps, "St", [128, DI], F32); StBh = [Buf(), Buf()]; HS = (H * 5) // 8
        accs = sb(ps, "accs", [128, DI], F32); accsB = Buf()
        Lt = [sb(ps, "Lt%d" % i, [128, DI], F32) for i in range(2)]; LtB = [Buf(), Buf()]; Ltd = [S.dsem(), S.dsem()]
        ett = [sb(ps, "ett%d" % i, [128, H2, 1], F32) for i in range(2)]; ettB = [Buf(), Buf()]; ettd = [S.dsem(), S.dsem()]
        Sbf = [sb(ps, "Sbf%d" % i, [128, DI], BF16) for i in range(2)]; SbfB = [Buf(), Buf()]; Sbd = [S.dsem(), S.dsem()]
        Pd = sb(ps, "Pd", [128, H2], F32); PdB = Buf()
        selb = sb(ps, "selb", [128, 8], F32); selB = Buf()
        aginB = Buf(); agoutB = Buf(); agd = S.dsem()
        S.dma("sp", selb[:, :], x.sel[0:1, :].partition_broadcast(128), [], [selB], agd)
        cnt = [0]
        St3 = St[:, :].rearrange("p (h q) -> p h q", q=64)

        def step(tile, d, store, Lsrc=None, Psrc=None, track=False):
            i = cnt[0] % 2; cnt[0] += 1
            if store:
                S.copy("act", Sbf[i][:, :], St[:, :], StBh, [SbfB[i]])
                r0 = (d * NT + tile) * 128
                S.dma("act", x.Sin[r0:r0 + 128, :], Sbf[i][:, :], [SbfB[i]], [], Sbd[i])
            if Lsrc is None:
                r0 = (d * NT + tile) * 128
                Lsrc = x.Ls[r0:r0 + 128, :]
                Psrc = x.etot[tile:tile + 1, :].partition_broadcast(128)
                rd = []
            else:
                rd = [agoutB]
            S.dma("sp", Lt[i][:, :], Lsrc, rd, [LtB[i]], Ltd[i])
            S.dma("sp", ett[i][:, :, 0], Psrc, rd, [ettB[i]], ettd[i])
            for hf, (eng, h0, h1) in enumerate((("dve", 0, HS), ("pool", HS, H))):
                S.tt(eng, St3[:, h0:h1, :], St3[:, h0:h1, :], bc(ett[i][:, d * H + h0:d * H + h1, :], [128, h1 - h0, 64]),
                     ALU.mult, [ettB[i]], [StBh[hf]])
                S.tt(eng, St[:, h0 * 64:h1 * 64], St[:, h0 * 64:h1 * 64], Lt[i][:, h0 * 64:h1 * 64], ALU.add, [LtB[i]],
                     [StBh[hf]])
            if track:
                S.tt("dve", Pd[:, d * H:(d + 1) * H], Pd[:, d * H:(d + 1) * H], ett[i][:, d * H:(d + 1) * H, 0], ALU.mult,
                     [ettB[i]], [PdB])

        for d in range(2):
            S.memset("pool", St[:, :], 0.0, StBh)
            for tile in (c.ownA if d == 0 else c.ownA[::-1]):
                step(tile, d, True)
        S.memset("dve", Pd[:, :], 1.0, [PdB])
        for d in range(2):
            S.memset("pool", St[:, :], 0.0, StBh)
            for tile in (c.ownB if d == 0 else c.ownB[::-1]):
                step(tile, d, False, track=True)
            S.dma("sp", x.agin[d * 128:(d + 1) * 128, :], St[:, :], StBh, [aginB], agd)
        S.dma("sp", x.agin[256:257, 0:H2], Pd[0:1, :], [PdB], [aginB], agd)
        agin, agout = x.agin, x.agout
        S.coll(lambda e: e.collective_compute("AllGather", ALU.bypass, replica_groups=[list(range(8))],
                                              ins=[agin.ap().opt()], outs=[agout.ap().opt()]),
               [aginB], [agoutB])
        for d in range(2):
            S.memset("pool", accs[:, :], 0.0, [accsB])
            for chain in ((0, 1, 2, 3), (4, 5, 6, 7)):
                S.memset("pool", St[:, :], 0.0, StBh)
                for p in (chain if d == 0 else chain[::-1]):
                    S.stt("dve", accs[:, :], St[:, :], selb[:, p:p + 1], accs[:, :], ALU.mult, ALU.add, StBh + [selB], [accsB])
                    step(None, d, False, Lsrc=agout[p * 257 + d * 128:p * 257 + (d + 1) * 128, :],
                         Psrc=agout[p * 257 + 256:p * 257 + 257, 0:H2].partition_broadcast(128))
            S.copy("dve", St[:, :], accs[:, :], [accsB], StBh)
            for tile in (c.ownB if d == 0 else c.ownB[::-1]):
                step(tile, d, True)
        S.barrier()
        S.flush()

    with ExitStack() as po:
        AH, KVH = c.AH, c.KVH
        biasm = sb(po, "biasm", [128, AH, 384], F32); biasB = Buf()
        with ExitStack() as ps:
            OH = sb(ps, "OH", [128, 33, 384], F32); rb = sb(ps, "rb", [128, 32 * AH], F32); ohB = Buf(); ohd = S.dsem()
            S.dma("sp", OH[:, :, :], x.ohm[:, :].rearrange("p (b k) -> p b k", k=384), [], [ohB], ohd)
            S.dma("sp", rb[:, :], x.relb[0:1, :].partition_broadcast(128), [], [ohB], ohd)
            hB = [Buf() for _ in range(AH)]
            for h in range(AH):
                eng = "dve"
                S.copy(eng, biasm[:, h, :], OH[:, 32, :], [ohB], [hB[h]])
                for b in range(32):
                    S.stt(eng, biasm[:, h, :], OH[:, b, :], rb[:, b * AH + h:b * AH + h + 1], biasm[:, h, :],
                          ALU.mult, ALU.add, [ohB], [hB[h]])
            S.barrier()
            S.flush()
        with ExitStack() as ps:
            t = load_small(ps)
            dsk3 = sb(ps, "dsk3", [128, H, 1], F32); snwb = sb(ps, "snwb", [128, DI], F32)
            sinkb = sb(ps, "sinkb", [128, AH], F32); hvb = sb(ps, "hvb", [128, 4], F32); pen = sb(ps, "pen", [128, 4], F32)
            ds = S.dsem()
            S.dma("sp", dsk3[:, :, 0], x.dskip[0:1, :].partition_broadcast(128), [], [t.B], ds)
            S.dma("sp", snwb[:, :], x.snw[0:1, :].partition_broadcast(128), [], [t.B], ds)
            S.dma("sp", sinkb[:, :], x.sink[0:1, :].partition_broadcast(128), [], [t.B], ds)
            S.dma("sp", hvb[:, :], x.hval[0:1, :].partition_broadcast(128), [], [t.B], ds)
            S.ts("dve", pen[:, :], hvb[:, :], -1.0, -NEG, ALU.add, ALU.mult, [], [t.B])
            xs = sb(ps, "xs", [128, CT, 128], BF16); xsB = Buf(); xsd = S.dsem()
            xtok = sb(ps, "xtok", [128, DI], BF16); xtokB = Buf()
            xdt = [sb(ps, "xdt%d" % d, [128, DI], BF16) for d in range(2)]; xdtB = [Buf(), Buf()]
            Sd = [sb(ps, "Sd%d" % d, [128, DI], BF16) for d in range(2)]; SdB = [Buf(), Buf()]; Sdd = [S.dsem(), S.dsem()]
            zt = sb(ps, "zt", [128, DI], BF16); ztB = Buf(); ztd = S.dsem()
            ecs3 = sb(ps, "ecs3", [128, H2, 1], F32); ecsB = Buf()
            CBm = [sb(ps, "CBm%d" % d, [128, G, 128], BF16) for d in range(2)]; CBmB = Buf()
            R0 = [sb(ps, "R%d" % d, [128, 1024], F32) for d in range(2)]; RB0 = [Buf(), Buf()]
            R_ = [R0, R0]; RB_ = [RB0, RB0]
            E_ = [[sb(ps, "E%d" % d, [128, 8, 128], BF16) for d in range(2)] for _ in range(2)]; EB_ = [[Buf(), Buf()] for _ in range(2)]
            MT_ = E_; MTB_ = EB_
            y = sb(ps, "y", [128, DI], F32)
            zg_ = [sb(ps, "zg", [128, 512], F32) for _ in range(2)]; zsB_ = [Buf(), Buf()]
            t1_ = [sb(ps, "t1", [128, 512], F32) for _ in range(2)]; t2_ = [sb(ps, "t2", [128, 512], F32) for _ in range(2)]
            t3s = sb(ps, "t3", [128, 512], F32); t3_ = [t3s, t3s]; t3Bs = Buf()
            t1B_ = [Buf(), Buf()]; t2B_ = [Buf(), Buf()]; t3B_ = [t3Bs, t3Bs]
            ygB_ = [Buf() for _ in range(G)]
            ms3 = sb(ps, "ms3", [128, G, 1], F32); msB = Buf()
            ynb = zt; ynbB = ztB
            ynTs = sb(ps, "ynTs", [128, DIt, 128], BF16); ynTB = Buf(); ynd = S.dsem()
            qTt = sb(ps, "qTt", [128, AH, 128], BF16); kTt = sb(ps, "kTt", [128, KVH, 384], BF16)
            vt = sb(ps, "vt", [128, 3, c.KVD], BF16); qkvB = Buf(); qkd = S.dsem()
            lg_ = [sb(ps, "lg", [128, 4, 384], F32) for _ in range(2)]; lgB_ = [Buf(), Buf()]
            pe_ = lg_; peB_ = lgB_
            st_ = [sb(ps, "st4", [128, 12, 1], F32) for _ in range(2)]; stB_ = [Buf(), Buf()]
            pn_ = [sb(ps, "pn", [128, 4, 384], BF16) for _ in range(2)]; pnB_ = [Buf(), Buf()]
            PnT_ = [sb(ps, "PnT", [128, 3, 512], BF16) for _ in range(2)]; PnTB_ = [Buf(), Buf()]
            atst = sb(ps, "atst", [128, AH, 128], BF16); atB = Buf(); atd = S.dsem()
            scale = float(128 ** -0.5)
            edge = {c.ownA[0]: (0, 0), c.ownA[-1]: (1, 256), c.ownB[0]: (2, 0), c.ownB[-1]: (3, 256)}
            for tile in c.own:
                tk = slice(tile * 128, (tile + 1) * 128)
                S.dma3("sp", xs[:, :, :], x.xcsT[:, tk].rearrange("(ct p) t -> p ct t", p=128), [], [xsB], xsd)
                for d in range(2):
                    r0 = (d * NT + tile) * 128
                    S.dma("sp", Sd[d][:, :], x.Sin[r0:r0 + 128, :], [], [SdB[d]], Sdd[d])
                S.dma("sp", zt[:, :], x.zs[tk, :], [], [ztB], ztd)
                transpose_to(lambda j: xs[:, j, :], DIt,
                             lambda j0, n: xtok[:, j0 * 128:(j0 + n) * 128].rearrange("p (k t) -> p k t", t=128), xsB, xtokB)
                dt_chunk(t, tile)
                S.act(ecs3[:, :, 0], t.cs4[:, 0:H2], AF.Exp, [t.dB], [ecsB])
                for d in range(2):
                    S.tt("dve" if d == 0 else "pool", xdt[d][:, :].rearrange("p (h q) -> p h q", q=64),
                         xtok[:, :].rearrange("p (h q) -> p h q", q=64), bc(t.dtv[:, d * H:(d + 1) * H, :], [128, H, 64]),
                         ALU.mult, [xtokB, t.dB], [xdtB[d]])
                for g0 in range(0, G, 4):
                    ng = min(4, G - g0)
                    pf, pfB = bankf()
                    S.mmv([(pf[:, i * 128:(i + 1) * 128], xs[:, DIt + g0 + i, :], xs[:, DIt + G + g0 + i, :], True, True)
                           for i in range(ng)], [xsB], [pfB])
                    pv = pf[:, 0:ng * 128].rearrange("p (g l) -> p g l", l=128)
                    S.tt("dve", CBm[0][:, g0:g0 + ng, :], pv, bc(U3, [128, ng, 128]), ALU.mult, [pfB, Bc], [CBmB])
                    S.tt("dve", CBm[1][:, g0:g0 + ng, :], pv, bc(UT3, [128, ng, 128]), ALU.mult, [pfB, Bc], [CBmB])
                S.dma3("sp", qTt[:, :, :], x.qT[:, tk].rearrange("(h d) t -> d h t", d=128), [], [qkvB], qkd)
                S.dma("sp", kTt[:, :, :], x.kT[:, (tile - 1) * 128:(tile + 2) * 128].rearrange("(h d) t -> d h t", d=128),
                      [], [qkvB], qkd)
                S.dma("sp", vt[:, :, :], x.vs[(tile - 1) * 128:(tile + 2) * 128, :].rearrange("(b p) n -> p b n", p=128),
                      [], [qkvB], qkd)
                def ssd_group(g):
                    gp = g % 2
                    R, RB, E, EB, MT, MTB = R_[gp], RB_[gp], E_[gp], EB_[gp], MT_[gp], MTB_[gp]
                    t1, t2, t3, t1B, t2B, t3B = t1_[gp], t2_[gp], t3_[gp], t1B_[gp], t2B_[gp], t3B_[gp]
                    zg, zsB = zg_[gp], zsB_[gp]
                    yB = ygB_[g]
                    for d in range(2):
                        S.tt("pool", R[d][:, :].rearrange("p (k l) -> p k l", l=128), bc(t.a3[:, d * H + g * 8:d * H + g * 8 + 8, :], [128, 8, 128]),
                             bc(U3 if d == 0 else UT3, [128, 8, 128]), ALU.mult, [t.dB, Bc], [RB[d]])
                        for hf in range(2):
                            pf, pfB = bankf()
                            S.mm(pf[:, 0:512], [(SUT if d == 0 else SU, R[d][:, hf * 512:(hf + 1) * 512])], True, True,
                                 [RB[d], Bc], [pfB])
                            S.act(E[d][:, hf * 4:(hf + 1) * 4, :], pf[:, 0:512].rearrange("p (k l) -> p k l", l=128), AF.Exp,
                                  [pfB], [EB[d]])
                        S.tt("dve", MT[d][:, :, :], E[d][:, :, :], bc(CBm[d][:, g:g + 1, :], [128, 8, 128]), ALU.mult,
                             [EB[d], CBmB], [MTB[d]])
                    yi, yiB = bankf()
                    items = []
                    for k in range(8):
                        hh = g * 8 + k
                        items.append((yi[:, k * 64:(k + 1) * 64], MT[0][:, k, :], xdt[0][:, hh * 64:(hh + 1) * 64], True, False))
                        items.append((yi[:, k * 64:(k + 1) * 64], MT[1][:, k, :], xdt[1][:, hh * 64:(hh + 1) * 64], False, True))
                    S.mmv(items, [MTB[0], MTB[1], xdtB[0], xdtB[1]], [yiB])
                    ysb = []
                    for d in range(2):
                        pf, pfB = bankf()
                        S.mm(pf[:, 0:512], [(xs[:, DIt + G + g, :], Sd[d][:, g * 512:(g + 1) * 512])], True, True,
                             [xsB, SdB[d]], [pfB])
                        ysb.append((pf, pfB))
                    v3 = lambda ap: ap.rearrange("p (k q) -> p k q", q=64)
                    yg = y[:, g * 512:(g + 1) * 512]
                    S.tt("dve", v3(t1[:, :]), v3(ysb[0][0][:, 0:512]), bc(ecs3[:, g * 8:g * 8 + 8, :], [128, 8, 64]), ALU.mult,
                         [ysb[0][1], ecsB], [t1B])
                    S.tt("dve", v3(t2[:, :]), v3(ysb[1][0][:, 0:512]), bc(ecs3[:, H + g * 8:H + g * 8 + 8, :], [128, 8, 64]),
                         ALU.mult, [ysb[1][1], ecsB], [t2B])
                    S.tt("dve", yg, yi[:, 0:512], t1[:, :], ALU.add, [yiB, t1B], [yB])
                    S.tt("pool", yg, yg, t2[:, :], ALU.add, [t2B], [yB])
                    S.tt("pool", v3(t3[:, :]), v3(xtok[:, g * 512:(g + 1) * 512]), bc(dsk3[:, g * 8:g * 8 + 8, :], [128, 8, 64]),
                         ALU.mult, [xtokB, t.B], [t3B])
                    S.tt("pool", yg, yg, t3[:, :], ALU.add, [t3B], [yB])
                    S.act(zg[:, :], zt[:, g * 512:(g + 1) * 512], AF.Silu, [ztB], [zsB])
                    S.tt("dve", yg, yg, zg[:, :], ALU.mult, [zsB], [yB])
                    S.tt("pool", zg[:, :], yg, yg, ALU.mult, [yB], [zsB])
                    S.op("dve", lambda e, g=g, zg=zg: e.reduce_sum(out=ms3[:, g, :], in_=zg[:, :], axis=AX.X), [zsB], [msB])
                def attn_group(kv):
                    kp = kv % 2
                    lg, lgB, pe32, peB, st4, stB4, pn, pnB, PnT, PnTB = (lg_[kp], lgB_[kp], pe_[kp], peB_[kp], st_[kp], stB_[kp],
                                                                         pn_[kp], pnB_[kp], PnT_[kp], PnTB_[kp])
                    for r in range(4):
                        h = kv * 4 + r
                        pf, pfB = bankf()
                        S.mm(pf[:, 0:384], [(qTt[:, h, :], kTt[:, kv, :])], True, True, [qkvB], [pfB])
                        S.stt("dve", lg[:, r, :], pf[:, 0:384], scale, biasm[:, h, :], ALU.mult, ALU.add, [pfB, biasB], [lgB])
                    if tile in edge:
                        pi, c0 = edge[tile]
                        S.ts("dve", lg[:, :, c0:c0 + 128], lg[:, :, c0:c0 + 128], pen[:, pi:pi + 1], None, ALU.add, None,
                             [t.B], [lgB])
                    S.op("dve", lambda e, lg=lg, st4=st4: e.reduce_max(out=st4[:, 0:4, 0], in_=lg[:, :, :], axis=AX.X), [lgB], [stB4])
                    S.tt("dve", st4[:, 0:4, 0], st4[:, 0:4, 0], sinkb[:, kv * 4:(kv + 1) * 4], ALU.max, [t.B], [stB4])
                    S.tt("dve", lg[:, :, :], lg[:, :, :], bc(st4[:, 0:4, :], [128, 4, 384]), ALU.subtract, [stB4], [lgB])
                    S.act(pe32[:, :, :], lg[:, :, :], AF.Exp, [lgB], [peB])
                    S.tt("dve", st4[:, 4:8, 0], sinkb[:, kv * 4:(kv + 1) * 4], st4[:, 0:4, 0], ALU.subtract, [t.B], [stB4])
                    S.act(st4[:, 4:8, 0], st4[:, 4:8, 0], AF.Exp, [], [stB4])
                    S.op("dve", lambda e, pe32=pe32, st4=st4: e.reduce_sum(out=st4[:, 8:12, 0], in_=pe32[:, :, :], axis=AX.X),
                         [peB], [stB4])
                    S.tt("dve", st4[:, 8:12, 0], st4[:, 8:12, 0], st4[:, 4:8, 0], ALU.add, [], [stB4])
                    S.op("dve", lambda e, st4=st4: e.reciprocal(out=st4[:, 8:12, 0], in_=st4[:, 8:12, 0]), [], [stB4])
                    S.tt("dve", pn[:, :, :], pe32[:, :, :], bc(st4[:, 8:12, :], [128, 4, 384]), ALU.mult, [peB, stB4], [pnB])
                    pa, paB = bankb()
                    S.tr([(pa[:, kb * 512 + r * 128:kb * 512 + (r + 1) * 128], pn[:, r, kb * 128:(kb + 1) * 128])
                          for kb in range(2) for r in range(4)], x.identb[:, :], [pnB, Bc], [paB])
                    S.copy("act", PnT[:, 0:2, :], pa[:, 0:1024].rearrange("p (b n) -> p b n", n=512), [paB], [PnTB])
                    pb_, pbB_ = bankb()
                    S.tr([(pb_[:, r * 128:(r + 1) * 128], pn[:, r, 256:384]) for r in range(4)], x.identb[:, :], [pnB, Bc], [pbB_])
                    S.copy("act", PnT[:, 2, :], pb_[:, 0:512], [pbB_], [PnTB])
                    pf, pfB = bankf()
                    S.mm(pf[:, 0:512], [(vt[:, kb, kv * 128:(kv + 1) * 128], PnT[:, kb, :]) for kb in range(3)], True, True,
                         [qkvB, PnTB], [pfB])
                    S.copy("act", atst[:, kv * 4:(kv + 1) * 4, :], pf[:, 0:512].rearrange("p (r q) -> p r q", q=128), [pfB], [atB])
                per = max(1, G // KVH)
                for g in range(G):
                    ssd_group(g)
                    if (g + 1) % per == 0 and (g + 1) // per - 1 < KVH:
                        attn_group((g + 1) // per - 1)
                for kv in range(G // per, KVH):
                    attn_group(kv)
                S.act(ms3[:, :, 0], ms3[:, :, 0], AF.Ln, [Bc], [msB], bias=x.epsb[:, 0:1], scale=1.0 / (DI // G))
                S.act(ms3[:, :, 0], ms3[:, :, 0], AF.Exp, [], [msB], scale=-0.5)
                S.tt("dve", y[:, :].rearrange("p (g q) -> p g q", g=G), y[:, :].rearrange("p (g q) -> p g q", g=G),
                     bc(ms3[:, :, :], [128, G, DI // G]), ALU.mult, [msB], ygB_)
                S.tt("pool", ynb[:, :], y[:, :], snwb[:, :], ALU.mult, ygB_ + [t.B], [ynbB])
                transpose_to(lambda j: ynb[:, j * 128:(j + 1) * 128], DIt, lambda j0, n: ynTs[:, j0:j0 + n, :], ynbB, ynTB)
                S.dma3("act", x.ynT[:, tk].rearrange("(ct p) t -> p ct t", p=128), ynTs[:, :, :], [ynTB], [], ynd)
                S.dma3("act", x.atT[:, tk].rearrange("(h d) t -> d h t", d=128), atst[:, :, :], [atB], [], atd)
            S.barrier()
            S.flush()


def build_phase_c(x):
    S, nc, c = x.S, x.nc, x.c
    sb, bankf, Bc = x.sb, x.bankf, x.Bc
    D, DI, DFF = c.D, c.DI, c.DFF
    KT = D // 128; DIt = DI // 128; FT = DFF // 128
    NA = max(DIt + KT, FT)
    WKT = x.WKT
    with ExitStack() as ps:
        W = x.WStream(ps, 5)
        A48 = sb(ps, "A48", [128, NA, 512], BF16); AB = Buf(); Ad = S.dsem()
        h1 = sb(ps, "h1", [128, 4, D], F32); hB = Buf(); hd = S.dsem(); od = S.dsem()
        gc = [sb(ps, "gc%d" % i, [128, 4, 512], BF16) for i in range(2)]; gcB = [Buf(), Buf()]; gcd = [S.dsem(), S.dsem()]
        tq = sb(ps, "tq", [128, 4, 512], F32); tqB = [Buf() for _ in range(4)]
        tq2 = sb(ps, "tq2", [128, 512], F32); tq2B = Buf()
        mg = sb(ps, "mg", [128, KT, 512], BF16); mgB = Buf()
        tmpb = sb(ps, "tmpbc", [128, 1, D], BF16); tmpB = Buf()
        ss = sb(ps, "ssc", [128, 8], F32); ssB = Buf()
        nwf = sb(ps, "nwf", [128, D], F32); nwo = sb(ps, "nwo", [128, D], F32); nwB = Buf(); nwd = S.dsem()
        S.dma("sp", nwf[:, :], x.normw[1:2, :].partition_broadcast(128), [], [nwB], nwd)
        S.dma("sp", nwo[:, :], x.normw[2:3, :].partition_broadcast(128), [], [nwB], nwd)
        sqv = tq[:, :, :].rearrange("p a b -> p (a b)")[:, 0:D]

        def stream(wsrc, Ktiles, c0, wd, mode, act_fn, actB, evac):
            nout = wd // 128 if mode == "ws" else 4
            banks = [bankf() for _ in range(nout)]
            panels = [(k0, min(WKT, Ktiles - k0)) for k0 in range(0, Ktiles, WKT)]
            for pi, (k0, nk) in enumerate(panels):
                wt, wB = W.load(wsrc, k0, nk, c0, wd)
                for u in range(nout):
                    if mode == "ws":
                        pairs = [(wt[:, j, u * 128:(u + 1) * 128], act_fn(k0 + j, None)) for j in range(nk)]
                        out = banks[u][0][:, 0:512]
                    else:
                        pairs = [(act_fn(k0 + j, u), wt[:, j, 0:wd]) for j in range(nk)]
                        out = banks[u][0][:, 0:wd]
                    S.mm(out, pairs, pi == 0, pi == len(panels) - 1, [actB, wB], [banks[u][1]])
            for u in range(nout):
                evac(u, banks[u][0], banks[u][1])

        own_blocks = [c.own[i:i + 4] for i in range(0, len(c.own), 4)]
        for bi, blk in enumerate(own_blocks):
            assert blk[3] == blk[0] + 3
            r0 = blk[0] * 128
            tk = slice(r0, r0 + 512)
            S.dma3("sp", A48[:, 0:DIt, :], x.ynT[:, tk].rearrange("(ct p) t -> p ct t", p=128), [], [AB], Ad)
            S.dma3("sp", A48[:, DIt:DIt + KT, :], x.atT[:, tk].rearrange("(ct p) t -> p ct t", p=128), [], [AB], Ad)
            S.dma("sp", h1[:, :, :], x.xin[tk, :].rearrange("(t p) d -> p t d", p=128), [], [hB], hd)
            for cc in range(D // 512):
                for gi in range(2):
                    S.dma("sp", gc[gi][:, :, :],
                          x.gT[gi * D + cc * 512:gi * D + (cc + 1) * 512, tk].rearrange("(u p) t -> p u t", p=128),
                          [], [gcB[gi]], gcd[gi])

                def ev_a(u, pf, pfB):
                    S.tt("dve", tq[:, u, :], pf[:, 0:512], gc[0][:, u, :], ALU.mult, [pfB, gcB[0]], [tqB[u]])
                stream(x.wb_ssm, DIt, cc * 512, 512, "ws", lambda k, u: A48[:, k, :], AB, ev_a)

                def ev_b(u, pf, pfB, cc=cc):
                    S.tt("dve", tq2[:, :], pf[:, 0:512], gc[1][:, u, :], ALU.mult, [pfB, gcB[1]], [tq2B])
                    S.tt("pool", mg[:, cc * 4 + u, :], tq[:, u, :], tq2[:, :], ALU.add, [tqB[u], tq2B], [mgB])
                stream(x.wb_attn, KT, cc * 512, 512, "ws", lambda k, u: A48[:, DIt + k, :], AB, ev_b)
            for cb in range(D // 512):
                def ev_o(tt, pf, pfB, cb=cb):
                    S.tt("dve", h1[:, tt, cb * 512:(cb + 1) * 512], pf[:, 0:512], h1[:, tt, cb * 512:(cb + 1) * 512], ALU.add,
                         [pfB], [hB])
                stream(x.wb_out, KT, cb * 512, 512, "as", lambda k, tt: mg[:, k, tt * 128:(tt + 1) * 128], mgB, ev_o)
            for tt in range(4):
                S.act(sqv, h1[:, tt, :], AF.Square, [hB], [tqB[0], tqB[1], tqB[2], tqB[3], ssB], scale=float(D) ** -0.5,
                      accum_out=ss[:, tt:tt + 1])
            x.rstd_ops(ss[:, 0:4], ssB)
            for tt in range(4):
                S.stt("dve", tmpb[:, 0, :], h1[:, tt, :], ss[:, tt:tt + 1], nwf[:, :], ALU.mult, ALU.mult, [hB, ssB, nwB], [tmpB])
                x.transpose_to(lambda j: tmpb[:, 0, j * 128:(j + 1) * 128], KT,
                               lambda j0, n, tt=tt: mg[:, j0:j0 + n, tt * 128:(tt + 1) * 128], tmpB, mgB)
            for fc in range(DFF // 512):
                def ev_g(u, pf, pfB):
                    S.act(tq[:, u, :], pf[:, 0:512], AF.Silu, [pfB], [tqB[u]])
                stream(x.wb_fi, KT, fc * 512, 512, "ws", lambda k, u: mg[:, k, :], mgB, ev_g)

                def ev_u(u, pf, pfB, fc=fc):
                    S.tt("dve", A48[:, fc * 4 + u, :], pf[:, 0:512], tq[:, u, :], ALU.mult, [pfB, tqB[u]], [AB])
                stream(x.wb_fi, KT, DFF + fc * 512, 512, "ws", lambda k, u: mg[:, k, :], mgB, ev_u)
            for cb in range(D // 512):
                def ev_f(tt, pf, pfB, cb=cb):
                    S.tt("dve", h1[:, tt, cb * 512:(cb + 1) * 512], pf[:, 0:512], h1[:, tt, cb * 512:(cb + 1) * 512], ALU.add,
                         [pfB], [hB])
                stream(x.wb_fo, FT, cb * 512, 512, "as", lambda k, tt: A48[:, k, tt * 128:(tt + 1) * 128], AB, ev_f)
            for tt in range(4):
                S.act(sqv, h1[:, tt, :], AF.Square, [hB], [tqB[0], tqB[1], tqB[2], tqB[3], ssB], scale=float(D) ** -0.5,
                      accum_out=ss[:, 4 + tt:5 + tt])
            x.rstd_ops(ss[:, 4:8], ssB)
            for tt in range(4):
                S.stt("dve", h1[:, tt, :], h1[:, tt, :], ss[:, 4 + tt:5 + tt], nwo[:, :], ALU.mult, ALU.mult, [ssB, nwB], [hB])
            o0 = bi * 512
            S.dma("act", x.y_out[o0:o0 + 512, :].rearrange("(t p) d -> p t d", p=128), h1[:, :, :], [hB], [], od)


def _buckets(rel):
    half = 16
    ret = (rel > 0).astype(np.int32) * half
    n = np.abs(rel)
    me = half // 2
    lg = me + (np.log(np.maximum(n, 1) / me) / np.log(128 / me) * (half - me)).astype(np.int32)
    lg = np.minimum(lg, half - 1)
    return ret + np.where(n < me, n, lg).astype(np.int32)


def host_consts():
    i = np.arange(128)
    ident = (i[:, None] == i[None, :])
    Um = (i[:, None] <= i[None, :])
    UTm = (i[:, None] >= i[None, :])
    SUm = (i[:, None] < i[None, :])
    SUTm = (i[:, None] > i[None, :])
    ones = np.ones((128, 128), bool)
    cst = np.concatenate([ident, Um, UTm, SUm, SUTm, ones], axis=1).astype(np.float32)
    rel = np.arange(384)[None, :] - 128 - np.arange(128)[:, None]
    bk = _buckets(rel)
    oh = np.zeros((128, 33, 384), np.float32)
    for b in range(32):
        oh[:, b, :] = (bk == b)
    oh[:, 32, :] = np.where(np.abs(rel) <= 128, 0.0, NEG)
    return cst, oh.reshape(128, 33 * 384)


def make_in_maps(cfg, inp):
    c = cfg
    cst, ohm = host_consts()
    f = lambda a: np.ascontiguousarray(a, dtype=np.float32)
    common = {
        "w_in": f(inp["w_in"][0]), "w_ssm": f(inp["w_ssm_branch"][0]), "w_attn": f(inp["w_attn_branch"][0]),
        "w_out": f(inp["w_out"][0]), "w_fi": f(inp["w_ffn_in"][0]), "w_fo": f(inp["w_ffn_out"][0]),
        "normw": f(np.stack([inp["mix_norm_w"][0], inp["ffn_norm_w"][0], inp["final_norm_w"]])),
        "conv_w": f(inp["conv_w"][0].reshape(5, c.CT, 128).transpose(2, 1, 0).reshape(128, c.CT * 5)),
        "conv_b": f(inp["conv_b"][0].reshape(c.CT, 128).T),
        "dtb": f(inp["dt_bias"][0].reshape(1, -1)), "alog": f(inp["a_log"][0].reshape(1, -1)),
        "dskip": f(inp["d_skip"][0].reshape(1, -1)), "snw": f(inp["ssm_norm_w"][0].reshape(1, -1)),
        "relb": f(inp["rel_bias"].reshape(1, -1)), "sink": f(inp["attn_sink"][0].reshape(1, -1)),
        "cst": cst, "ohm": ohm,
    }
    xp, xs = inp["x_prompt"], inp["x_sample"]
    maps = []
    for core in range(8):
        s, qd = core // 4, core % 4
        xin = np.zeros((c.NTOK, c.D), np.float32)
        xin[128:128 + c.SEQ] = xp[core]
        b0 = (c.TA + 2) * 128
        lo = qd * c.QL - 128
        hi = (qd + 1) * c.QL + 128
        slo, shi = max(lo, 0), min(hi, c.DEC_SEQ)
        xin[b0 + (slo - lo): b0 + (slo - lo) + (shi - slo)] = xs[s, slo:shi]
        sel = np.zeros((1, 8), np.float32); sel[0, core] = 1.0
        hval = np.array([[0.0, 0.0, 1.0 if qd > 0 else 0.0, 1.0 if qd < 3 else 0.0]], np.float32)
        m = dict(common)
        m.update({"xin": xin, "sel": sel, "hval": hval})
        maps.append(m)
    return maps


def run(cfg, inp):
    nc = build(cfg)
    maps = make_in_maps(cfg, inp)
    res = run_bass_kernel_spmd(nc, maps, core_ids=list(range(8)))
    c = cfg
    yp = np.zeros((8, c.SEQ, c.D), np.float32)
    ys = np.zeros((2, c.DEC_SEQ, c.D), np.float32)
    for core in range(8):
        y = res.results[core]["y_out"]
        yp[core] = y[:c.SEQ]
        s, qd = core // 4, core % 4
        ys[s, qd * c.QL:(qd + 1) * c.QL] = y[c.SEQ:]
    return (yp, ys), res


def kernel(**inputs):
    out, _ = run(CFG_FULL, inputs)
    return out
```
